# Optimizing a Trainium2 kernel written in Bass

```python
import jax, jax.numpy as jnp
from jax import lax
import numpy as np

D_MODEL = 1024
BATCH = 8
SEQ = 2048
DEPTH = 2
DEC_BATCH = 128
DEC_SEQ = 4
PAST_LEN = 16384
PAGE_SIZE = 128

N_BRANCH = 4
W_BRANCH = D_MODEL // 2
W_A = W_BRANCH
W_B = W_BRANCH
W_C = W_BRANCH
W_D = W_BRANCH
CONV_A_WIDTH = 31
CONV_B_WIDTH = 3
CHUNK = 128
C_GROUPS = 4
C_GROUP_DIM = W_C // C_GROUPS
POOL_WINDOWS = (2, 4, 8, 16)
N_POOL = len(POOL_WINDOWS)
D_GROUP_DIM = W_D // N_POOL
POOL_BUF = max(POOL_WINDOWS) - 1
N_MEM = 256
X_HEADS = 4
X_HEAD_DIM = D_MODEL // X_HEADS
D_FF = 2816
FFN_CONV_WIDTH = 3
EPS = 1e-6
SPLIT_SIZES = (W_A, W_A, W_B, W_B, W_B, 2 * W_C, W_D, N_BRANCH * D_MODEL)
IN_COLS = 2 * W_A + 3 * W_B + 2 * W_C + W_D + N_BRANCH * D_MODEL

kernel_name = "gated_parallel_conv_chunkmlp_pool_decoder_step"


def rmsnorm(x, g):
    xf = x.astype(jnp.float32)
    y = xf * lax.rsqrt(jnp.mean(xf * xf, axis=-1, keepdims=True) + EPS)
    return (y * g.astype(jnp.float32)).astype(x.dtype)


def layernorm(x, g, b):
    xf = x.astype(jnp.float32)
    mu = jnp.mean(xf, axis=-1, keepdims=True)
    var = jnp.mean(jnp.square(xf - mu), axis=-1, keepdims=True)
    return ((xf - mu) * lax.rsqrt(var + EPS) * g.astype(jnp.float32) + b.astype(jnp.float32)).astype(x.dtype)


def causal_dwconv(x, buf, w, b):
    k, c = w.shape
    xc = jnp.concatenate([buf.astype(x.dtype), x], axis=1)
    y = lax.conv_general_dilated(xc, w[:, None, :].astype(x.dtype), window_strides=(1,), padding='VALID',
                                 dimension_numbers=('NWC', 'WIO', 'NWC'), feature_group_count=c)
    return y + b.astype(x.dtype), xc[:, xc.shape[1] - (k - 1):]


def causal_multiscale_pool(x, buf, past):
    t = x.shape[1]
    xc = jnp.concatenate([buf.astype(x.dtype), x], axis=1)
    cs = jnp.cumsum(xc.astype(jnp.float32), axis=1)
    cs = jnp.concatenate([jnp.zeros_like(cs[:, :1]), cs], axis=1)
    end = cs[:, POOL_BUF + 1:]
    pos = past + jnp.arange(t)
    outs = []
    for gi, w in enumerate(POOL_WINDOWS):
        sl = slice(gi * D_GROUP_DIM, (gi + 1) * D_GROUP_DIM)
        start = cs[:, POOL_BUF + 1 - w: POOL_BUF + 1 - w + t, sl]
        cnt = jnp.minimum(w, pos + 1).astype(jnp.float32)[None, :, None]
        outs.append((end[..., sl] - start) / cnt)
    pooled = jnp.concatenate(outs, axis=-1)
    return (pooled - x.astype(jnp.float32)).astype(x.dtype), xc[:, xc.shape[1] - POOL_BUF:]


def chunk_spatial_gate(u, v, w_s, b_s):
    n, t, _ = v.shape
    L = min(t, CHUNK)
    nc = t // L
    mask = jnp.tril(jnp.ones((L, L), dtype=bool))
    ws = jnp.where(mask[None], w_s[:, :L, :L], 0.0).astype(v.dtype)
    vc = v.reshape(n, nc, L, C_GROUPS, C_GROUP_DIM)
    s = jnp.einsum('gts,ncsgd->nctgd', ws, vc) + b_s[:, :L].T.astype(v.dtype)[None, None, :, :, None]
    return u * s.reshape(n, t, W_C)


def decoder_layer(x, mem_k, mem_v, buf_a, buf_b, buf_d, buf_f, past, lw):
    (g_mix, w_in, conv_a_w, conv_a_b, ln_a_g, ln_a_b, w_a_out, conv_b_w, conv_b_b, w_b_out,
     ln_c_g, ln_c_b, w_s, b_s, w_c_out, w_d_grp, scale_d, w_d_out, w_mix_out,
     g_xattn, w_q, w_xo, g_ffn, w_up, ffn_conv_w, ffn_conv_b, w_down) = lw
    n, t, _ = x.shape
    h = rmsnorm(x, g_mix)
    z = h @ w_in
    points = np.cumsum(SPLIT_SIZES)[:-1].tolist()
    a_val, a_gate, gb, gc, hb, uv, xd, gate_pre = jnp.split(z, points, axis=-1)
    a, new_a = causal_dwconv(a_val * jax.nn.sigmoid(a_gate), buf_a, conv_a_w, conv_a_b)
    y_a = jax.nn.silu(layernorm(a, ln_a_g, ln_a_b)) @ w_a_out
    cb, new_b = causal_dwconv(gc * hb, buf_b, conv_b_w, conv_b_b)
    y_b = (gb * cb) @ w_b_out
    u, v = jnp.split(jax.nn.gelu(uv), 2, axis=-1)
    v = layernorm(v, ln_c_g, ln_c_b)
    y_c = chunk_spatial_gate(u, v, w_s, b_s) @ w_c_out
    pd, new_d = causal_multiscale_pool(xd, buf_d, past)
    pd = jnp.einsum('ntgc,gcd->ntgd', pd.reshape(n, t, N_POOL, D_GROUP_DIM), w_d_grp).reshape(n, t, W_D)
    y_d = (pd * scale_d) @ w_d_out
    g_a, g_b, g_c, g_d = jnp.split(jax.nn.sigmoid(gate_pre), N_BRANCH, axis=-1)
    m = g_a * y_a + g_b * y_b + g_c * y_c + g_d * y_d
    x = x + m @ w_mix_out
    q = (rmsnorm(x, g_xattn) @ w_q).reshape(n, t, X_HEADS, X_HEAD_DIM)
    s = jnp.einsum('nthd,nmhd->nhtm', q, mem_k.astype(q.dtype), preferred_element_type=jnp.float32)
    p = jax.nn.softmax(s * (X_HEAD_DIM ** -0.5), axis=-1)
    o = jnp.einsum('nhtm,nmhd->nthd', p.astype(x.dtype), mem_v.astype(x.dtype)).reshape(n, t, D_MODEL)
    x = x + o @ w_xo
    up, new_f = causal_dwconv(rmsnorm(x, g_ffn) @ w_up, buf_f, ffn_conv_w, ffn_conv_b)
    fa, fb = jnp.split(up, 2, axis=-1)
    x = x + (jax.nn.silu(fa) * fb) @ w_down
    return x, new_a, new_b, new_d, new_f, v


def setup_inputs(seed: int = 0) -> dict:
    key = jax.random.key(seed)
    ks = iter(jax.random.split(key, 64))
    f32 = jnp.float32

    def nrm(shape, scale):
        return jax.random.normal(next(ks), shape, f32) * scale

    def gain(shape):
        return 1.0 + 0.01 * jax.random.normal(next(ks), shape, f32)

    L = DEPTH
    return {
        "x_prompt": nrm((BATCH, SEQ, D_MODEL), 1.0),
        "x_sample": nrm((DEC_BATCH, DEC_SEQ, D_MODEL), 1.0),
        "mem_prompt": nrm((BATCH, N_MEM, D_MODEL), 1.0),
        "state_conv_a": nrm((L, DEC_BATCH, CONV_A_WIDTH - 1, W_A), 1.0),
        "state_conv_b": nrm((L, DEC_BATCH, CONV_B_WIDTH - 1, W_B), 1.0),
        "state_pool_d": nrm((L, DEC_BATCH, POOL_BUF, W_D), 1.0),
        "state_ffn_conv": nrm((L, DEC_BATCH, FFN_CONV_WIDTH - 1, 2 * D_FF), 1.0),
        "cache_mem_k": nrm((L, DEC_BATCH, N_MEM, X_HEADS, X_HEAD_DIM), 1.0),
        "cache_mem_v": nrm((L, DEC_BATCH, N_MEM, X_HEADS, X_HEAD_DIM), 1.0),
        "g_mix": gain((L, D_MODEL)),
        "w_in": nrm((L, D_MODEL, IN_COLS), D_MODEL ** -0.5),
        "conv_a_w": nrm((L, CONV_A_WIDTH, W_A), CONV_A_WIDTH ** -0.5),
        "conv_a_b": nrm((L, W_A), 0.01),
        "ln_a_g": gain((L, W_A)),
        "ln_a_b": nrm((L, W_A), 0.01),
        "w_a_out": nrm((L, W_A, D_MODEL), W_A ** -0.5),
        "conv_b_w": nrm((L, CONV_B_WIDTH, W_B), CONV_B_WIDTH ** -0.5),
        "conv_b_b": nrm((L, W_B), 0.01),
        "w_b_out": nrm((L, W_B, D_MODEL), W_B ** -0.5),
        "ln_c_g": gain((L, W_C)),
        "ln_c_b": nrm((L, W_C), 0.01),
        "w_s": nrm((L, C_GROUPS, CHUNK, CHUNK), CHUNK ** -0.5),
        "b_s": 1.0 + nrm((L, C_GROUPS, CHUNK), 0.1),
        "w_c_out": nrm((L, W_C, D_MODEL), W_C ** -0.5),
        "w_d_grp": nrm((L, N_POOL, D_GROUP_DIM, D_GROUP_DIM), D_GROUP_DIM ** -0.5),
        "scale_d": gain((L, W_D)),
        "w_d_out": nrm((L, W_D, D_MODEL), W_D ** -0.5),
        "w_mix_out": nrm((L, D_MODEL, D_MODEL), D_MODEL ** -0.5),
        "g_xattn": gain((L, D_MODEL)),
        "w_q": nrm((L, D_MODEL, D_MODEL), D_MODEL ** -0.5),
        "w_mk": nrm((L, D_MODEL, D_MODEL), D_MODEL ** -0.5),
        "w_mv": nrm((L, D_MODEL, D_MODEL), D_MODEL ** -0.5),
        "w_xo": nrm((L, D_MODEL, D_MODEL), D_MODEL ** -0.5),
        "g_ffn": gain((L, D_MODEL)),
        "w_up": nrm((L, D_MODEL, 2 * D_FF), D_MODEL ** -0.5),
        "ffn_conv_w": nrm((L, FFN_CONV_WIDTH, 2 * D_FF), FFN_CONV_WIDTH ** -0.5),
        "ffn_conv_b": nrm((L, 2 * D_FF), 0.01),
        "w_down": nrm((L, D_FF, D_MODEL), D_FF ** -0.5),
        "g_final": gain((D_MODEL,)),
    }


def reference(x_prompt, x_sample, mem_prompt, state_conv_a, state_conv_b, state_pool_d, state_ffn_conv,
              cache_mem_k, cache_mem_v, g_mix, w_in, conv_a_w, conv_a_b, ln_a_g, ln_a_b, w_a_out,
              conv_b_w, conv_b_b, w_b_out, ln_c_g, ln_c_b, w_s, b_s, w_c_out, w_d_grp, scale_d, w_d_out,
              w_mix_out, g_xattn, w_q, w_mk, w_mv, w_xo, g_ffn, w_up, ffn_conv_w, ffn_conv_b, w_down, g_final):
    nb = x_prompt.shape[0]
    xp, xs = x_prompt, x_sample
    a_p, b_p, d_p, f_p, v_p, mk_p, mv_p = [], [], [], [], [], [], []
    a_s, b_s_new, d_s, f_s, v_s = [], [], [], [], []
    for l in range(DEPTH):
        lw = (g_mix[l], w_in[l], conv_a_w[l], conv_a_b[l], ln_a_g[l], ln_a_b[l], w_a_out[l],
              conv_b_w[l], conv_b_b[l], w_b_out[l], ln_c_g[l], ln_c_b[l], w_s[l], b_s[l], w_c_out[l],
              w_d_grp[l], scale_d[l], w_d_out[l], w_mix_out[l], g_xattn[l], w_q[l], w_xo[l],
              g_ffn[l], w_up[l], ffn_conv_w[l], ffn_conv_b[l], w_down[l])
        mk = (mem_prompt @ w_mk[l]).reshape(nb, N_MEM, X_HEADS, X_HEAD_DIM)
        mv = (mem_prompt @ w_mv[l]).reshape(nb, N_MEM, X_HEADS, X_HEAD_DIM)
        zdt = xp.dtype
        xp, na, nbf, nd, nf, vv = decoder_layer(
            xp, mk, mv,
            jnp.zeros((nb, CONV_A_WIDTH - 1, W_A), zdt), jnp.zeros((nb, CONV_B_WIDTH - 1, W_B), zdt),
            jnp.zeros((nb, POOL_BUF, W_D), zdt), jnp.zeros((nb, FFN_CONV_WIDTH - 1, 2 * D_FF), zdt),
            0, lw)
        a_p.append(na); b_p.append(nbf); d_p.append(nd); f_p.append(nf)
        v_p.append(vv[:, vv.shape[1] - CHUNK:]); mk_p.append(mk); mv_p.append(mv)
        xs, na, nbf, nd, nf, vv = decoder_layer(
            xs, cache_mem_k[l], cache_mem_v[l], state_conv_a[l], state_conv_b[l], state_pool_d[l],
            state_ffn_conv[l], PAST_LEN, lw)
        a_s.append(na); b_s_new.append(nbf); d_s.append(nd); f_s.append(nf); v_s.append(vv)
    y_prompt = rmsnorm(xp, g_final)
    y_sample = rmsnorm(xs, g_final)
    return (y_prompt, y_sample,
            jnp.stack(a_p), jnp.stack(b_p), jnp.stack(d_p), jnp.stack(f_p), jnp.stack(v_p),
            jnp.stack(mk_p), jnp.stack(mv_p),
            jnp.stack(a_s), jnp.stack(b_s_new), jnp.stack(d_s), jnp.stack(f_s), jnp.stack(v_s))
```

```python
import contextlib
import numpy as np
import concourse.bass as bass
import concourse.mybir as mybir
from concourse.bass_utils import run_bass_kernel_spmd

F32 = mybir.dt.float32
BF16 = mybir.dt.bfloat16
AF = mybir.ActivationFunctionType
ALU = mybir.AluOpType

ENGS = ("pe", "act", "dve", "pool", "sp")
SEM_ROT = 2500
SELF_DIST = 16


class Prog:
    def __init__(self, nc):
        self.nc = nc
        self.ops = {e: [] for e in ENGS}
        self.last_write = {}
        self.readers = {}
        self.seen = {e: {} for e in ENGS}
        self.dma_cnt = {}
        self.dma_keys = []
        self.last_c = {e: -1 for e in ENGS}
        self.tag = ""
        self.know = {e: {} for e in ENGS}
        self.opknow = {}

    def _need(self, eng, tok, waits, is_dma_issue=False, my_dma_key=None):
        if tok[0] == "e":
            _, p, i = tok
            if p == eng:
                if eng in ("act", "dve", "pool") and (is_dma_issue or i >= len(self.ops[eng]) - SELF_DIST):
                    if self.seen[eng].get(p, -1) >= i:
                        return
                    self.seen[eng][p] = i
                    waits.append(tok)
                    self.ops[p][i]["signal"] = True
                return
            if self.seen[eng].get(p, -1) >= i or self.know[eng].get(p, -1) >= i:
                return
            self.seen[eng][p] = i
            waits.append(tok)
            self.ops[p][i]["signal"] = True
            k = self.know[eng]
            for q, v in self.opknow.get((p, i), {}).items():
                if q != eng and k.get(q, -1) < v:
                    k[q] = v
            if k.get(p, -1) < i:
                k[p] = i
        else:
            _, key, cnt = tok
            if my_dma_key is not None and key == my_dma_key:
                return
            cnt = self.dma_cnt[key]
            if self.seen[eng].get(key, 0) >= cnt:
                return
            self.seen[eng][key] = cnt
            waits.append(("d", key, cnt))

    def _deps(self, eng, reads, writes, is_dma_issue=False, my_dma_key=None):
        waits = []
        for r in reads:
            t = self.last_write.get(r)
            if t is not None:
                self._need(eng, t, waits, is_dma_issue, my_dma_key)
        for r in writes:
            t = self.last_write.get(r)
            if t is not None:
                self._need(eng, t, waits, is_dma_issue, my_dma_key)
            for t in self.readers.get(r, {}).values():
                self._need(eng, t, waits, is_dma_issue, my_dma_key)
        return waits

    def _commit(self, tok, reads, writes, rkey):
        for r in reads:
            self.readers.setdefault(r, {})[rkey] = tok
        for r in writes:
            self.last_write[r] = tok
            self.readers[r] = {}

    def op(self, eng, fn, reads=(), writes=()):
        waits = self._deps(eng, reads, writes)
        idx = len(self.ops[eng])
        self.ops[eng].append(dict(kind="c", fn=fn, waits=waits, signal=False, tag=self.tag))
        self.last_c[eng] = idx
        self.opknow[(eng, idx)] = dict(self.know[eng])
        self._commit(("e", eng, idx), reads, writes, eng)

    def dma(self, eng, fn, key, reads=(), writes=()):
        if key not in self.dma_cnt:
            self.dma_cnt[key] = 0
            self.dma_keys.append(key)
        waits = self._deps(eng, reads, writes, is_dma_issue=True, my_dma_key=key)
        self.dma_cnt[key] += 16
        self.ops[eng].append(dict(kind="d", fn=fn, waits=waits, signal=False, key=key))
        self._commit(("d", key, self.dma_cnt[key]), reads, writes, key)

    def wait_all(self, eng):
        waits = []
        for k in self.dma_keys:
            if self.dma_cnt[k] > 0:
                self._need(eng, ("d", k, self.dma_cnt[k]), waits)
        self.ops[eng].append(dict(kind="w", fn=None, waits=waits, signal=False))

    def emit(self):
        nc = self.nc
        with contextlib.ExitStack() as st:
            sigval, esems = {}, {}
            for e in ENGS:
                c = 0
                for i, o in enumerate(self.ops[e]):
                    if o["signal"]:
                        c += 1
                        sigval[(e, i)] = c
                nsem = (c + SEM_ROT - 1) // SEM_ROT
                esems[e] = [st.enter_context(nc.semaphore(f"s_{e}{k}")) for k in range(nsem)]
            dsems = {k: st.enter_context(nc.semaphore(f"d_{k}")) for k in self.dma_keys}

            def resolve(tok):
                if tok[0] == "e":
                    v = sigval[(tok[1], tok[2])]
                    return esems[tok[1]][(v - 1) // SEM_ROT], (v - 1) % SEM_ROT + 1
                return dsems[tok[1]], tok[2]

            block = st.enter_context(nc.Block())

            def run(engname):
                def body(eng):
                    for i, o in enumerate(self.ops[engname]):
                        for t in o["waits"]:
                            s, v = resolve(t)
                            eng.wait_ge(s, v)
                        if o["kind"] == "w":
                            continue
                        inst = o["fn"](eng)
                        if o["kind"] == "d":
                            inst.then_inc(dsems[o["key"]], 16)
                        elif o["signal"]:
                            s, v = resolve(("e", engname, i))
                            inst.then_inc(s, 1)
                return body

            block.tensor(run("pe"))
            block.scalar(run("act"))
            block.vector(run("dve"))
            block.gpsimd(run("pool"))
            block.sync(run("sp"))


D = 1024
NL = 2
SEQ = 2048
NT = 512
NS = 16
TS = 4
DFF = 2816
EPS = 1e-6
BLK = 560
NSLOT = 5
SLOTW = 4096

_PP_FIELDS = [("g_mix", 8), ("g_xattn", 8), ("g_ffn", 8), ("conv_a_w", 124), ("conv_a_b", 4), ("ln_a_g", 4),
              ("ln_a_b", 4), ("conv_b_w", 12), ("conv_b_b", 4), ("ln_c_g", 4), ("ln_c_b", 4), ("scale_d", 4),
              ("ffn_conv_w", 132), ("ffn_conv_b", 44)]
PPO = {}
_o = 0
for _n, _w in _PP_FIELDS:
    PPO[_n] = _o
    _o += _w
PPL = _o
NPP = PPL * NL + 8


class Seg:
    def __init__(self, kind, S, T, c0, t0=0, s0=0, prevT=None, last=False):
        self.kind, self.S, self.T, self.c0 = kind, S, T, c0
        self.n = S * T
        self.t0, self.s0, self.prevT, self.last = t0, s0, prevT, last

    def ebase(self, H):
        return 0 if self.prevT is None else H + self.prevT


class TC:
    def __init__(self, idx, segs):
        self.idx = idx
        self.segs = segs
        self.N = sum(s.n for s in segs)

    def extlen(self, H):
        return sum(s.S * (H + s.T) for s in self.segs)


def build_program(tile_sel=None, nlayers=NL):
    nc = bass.Bass("TRN2", target_bir_lowering=False)

    def din(name, shape):
        return nc.dram_tensor(name, list(shape), F32, kind="ExternalInput").ap()

    def dout(name, shape):
        return nc.dram_tensor(name, list(shape), F32, kind="ExternalOutput").ap()

    xpT = din("xpT", [128, 8, SEQ])
    xsT = din("xsT", [128, 8, NS * TS])
    memT_d = din("memT", [128, 8, 256])
    st_a = din("st_a", [NL, 128, 4, NS, 30])
    st_b = din("st_b", [NL, 128, 4, NS, 2])
    st_d = din("st_d", [NL, 128, 4, NS, 15])
    st_f = din("st_f", [NL, 128, 2, 44, NS // 2, 2])
    ckT = din("ckT", [NL, NS, 128, 8, 256])
    cvN = din("cvN", [NL, NS, 128, 2, 1024])
    w_in = din("w_in", [NL, D, 8192])
    w_a_out = din("w_a_out", [NL, 512, D])
    w_b_out = din("w_b_out", [NL, 512, D])
    w_c_out = din("w_c_out", [NL, 512, D])
    w_d_out = din("w_d_out", [NL, 512, D])
    w_mix = din("w_mix_out", [NL, D, D])
    w_q = din("w_q", [NL, D, D])
    w_mk = din("w_mk", [NL, D, D])
    w_mv = din("w_mv", [NL, D, D])
    w_xo = din("w_xo", [NL, D, D])
    w_up = din("w_up", [NL, D, 2 * DFF])
    w_down = din("w_down", [NL, DFF, D])
    pp_d = din("pp", [128, NPP])
    wsT_d = din("wsT", [NL, 128, 4, 128])
    wsS_d = din("wsS", [NL, 64, 4, 64])
    bsb_d = din("bsb", [NL, 128, 4, 128])
    bss_d = din("bss", [NL, 128, 4, 64])
    wdg_d = din("wdg", [NL, 128, 4, 128])

    o_yp = dout("o_yp", [128, 8, SEQ])
    o_ys = dout("o_ys", [128, 8, NS * TS])
    o_ap = dout("o_ap", [NL, 128, 4, 30])
    o_bp = dout("o_bp", [NL, 128, 4, 2])
    o_dp = dout("o_dp", [NL, 128, 4, 15])
    o_fp = dout("o_fp", [NL, 128, 44, 2])
    o_vp = dout("o_vp", [NL, 128, 4, 128])
    o_mk = dout("o_mk", [NL, 128, 8, 256])
    o_mv = dout("o_mv", [NL, 128, 2, 1024])
    o_as = dout("o_as", [NL, 128, 4, NS, 30])
    o_bs = dout("o_bs", [NL, 128, 4, NS, 2])
    o_ds = dout("o_ds", [NL, 128, 4, NS, 15])
    o_fs = dout("o_fs", [NL, 128, 2, 44, NS // 2, 2])
    o_vs = dout("o_vs", [NL, 128, 4, NS * TS])

    st = contextlib.ExitStack()
    with st:
        def sb(name, shape, dt=F32):
            return st.enter_context(nc.sbuf_tensor(name, list(shape), dt))

        XT = sb("XT", [128, 8, NT])
        HB = sb("HB", [128, 8, NT], BF16)
        SQ = sb("SQ", [128, 8, NT], BF16)
        PPt = sb("PP", [128, NPP])
        IDENT = sb("IDENT", [128, 128])
        ONES = sb("ONES", [128, 3, 128], BF16)
        DUMMY = sb("DUMMY", [128, 2])
        KT = sb("KT", [128, NL, 8, 256], BF16)
        VT = sb("VT", [128, NL, 2, 1024], BF16)
        WDG = sb("WDG", [128, NL, 4, 128], BF16)
        WST = sb("WST", [128, NL, 4, 128], BF16)
        WSS = sb("WSS", [64, NL, 4, 64], BF16)
        BSB = sb("BSB", [128, NL, 4, 128])
        BSS = sb("BSS", [128, NL, 4, 64])
        HA = sb("HA", [128, NL, 4, 30])
        HBh = sb("HBh", [128, NL, 4, 2])
        HD = sb("HD", [128, NL, 4, 15])
        HF = sb("HF", [128, NL, 44, 2])
        HFS = sb("HFS", [128, 2, 44, NS // 2, 2])
        INVC = sb("INVC", [128, 4, 15])
        STAT = [sb(f"STAT{i}", [128, NT]) for i in range(2)]
        GS = [sb(f"GS{i}", [128, NT]) for i in range(4)]
        MB = sb("MB", [128, 8, NT], BF16)
        VNT = sb("VNT", [128, 4, 4, 128], BF16)
        YIN = [sb(f"YIN{i}", [128, 4, NT], BF16) for i in range(4)]
        WS = [sb(f"WS{i}", [128, SLOTW], BF16) for i in range(NSLOT)]
        SB_ = [sb(f"S{i}", [128, 4 * BLK]) for i in range(5)]
        SBb = [t.bitcast(BF16) for t in SB_]
        STS = sb("STS", [128, 4 * 8 * 30])
        STG = sb("STG", [128, 8, 256])

        PS = [st.enter_context(nc.psum_tensor(f"PS{i}", [128, 512], F32)) for i in range(8)]

        P = Prog(nc)
        cnt = dict(ps=0, slot=0, gs=0, scr=0)

        def R(name, n):
            return [f"{name}{i}" for i in range(n)]

        def next_ps():
            b = cnt["ps"] % 8
            cnt["ps"] += 1
            return b

        def next_gs():
            b = cnt["gs"] % 4
            cnt["gs"] += 1
            return b

        def mm(ps_ap, pairs, reads, psreg):
            pairs = list(pairs)

            def fn(e):
                n = len(pairs)
                ins = None
                for i, (l, r) in enumerate(pairs):
                    ins = e.matmul(ps_ap, l, r, start=(i == 0), stop=(i == n - 1))
                return ins
            P.op("pe", fn, reads=reads, writes=[psreg])

        def load_w(src_ap, kc, ncols):
            s = cnt["slot"] % NSLOT
            cnt["slot"] += 1
            view = WS[s][:, 0:kc * ncols].rearrange("p (k n) -> p k n", k=kc)
            P.dma("pool", lambda e: e.dma_start(out=view, in_=src_ap), key=f"w{s}", writes=[f"ws{s}"])
            return view, f"ws{s}"

        def wsrc(w_l, c0, ncols):
            return w_l.rearrange("(k p) n -> p k n", p=128)[:, :, c0:c0 + ncols]

        def ppc(l, name, j, w=1):
            o = l * PPL + PPO[name] + j
            return PPt[:, o:o + w]

        def sq_(blk, c):
            return f"S{blk}q{c}"

        def sblk(blk):
            return [sq_(blk, c) for c in range(4)]

        def flat(tc, blk, c):
            return SB_[blk][:, c * BLK: c * BLK + tc.N]

        def quarters(blk, a, b, nch=4):
            return SB_[blk][:, 0:4 * BLK].rearrange("p (c x) -> p c x", c=4)[:, 0:nch, a:b]

        def fseg(sg, ap):
            a = ap[:, sg.c0:sg.c0 + sg.n]
            return a if sg.S == 1 else a.rearrange("p (s t) -> p s t", s=sg.S)

        def ext(sg, blk, c, H, a, b):
            L = H + sg.T
            o = c * BLK + sg.ebase(H)
            base = SB_[blk][:, o:o + sg.S * L]
            if sg.S == 1:
                return base[:, a:b]
            return base.rearrange("p (s j) -> p s j", s=sg.S)[:, :, a:b]

        def bf8(blk, j, n):
            o = (j // 2) * (2 * BLK) + (j % 2) * 512
            return SBb[blk][:, o:o + n]

        def bf8r(blk, j):
            return sq_(blk, j // 2)

        P.op("pool", lambda e: e.memset(IDENT[:], 0.0), writes=["ident"])
        P.op("pool", lambda e: e.affine_select(out=IDENT[:], in_=IDENT[:], compare_op=ALU.not_equal, fill=1.0,
                                               base=0, pattern=[[-1, 128]], channel_multiplier=1),
             reads=["ident"], writes=["ident"])
        P.op("dve", lambda e: e.memset(ONES[:, 0, :], 1.0 / 1024), writes=["ones"])
        P.op("dve", lambda e: e.memset(ONES[:, 1, :], 1.0 / 512), writes=["ones"])
        P.op("dve", lambda e: e.memset(ONES[:, 2, :], 1.0), writes=["ones"])
        for t_, nm in ((HA, "ha"), (HBh, "hb_"), (HD, "hd"), (HF, "hf")):
            P.op("dve", lambda e, t_=t_: e.memset(t_[:], 0.0), writes=[nm + "0", nm + "1"])
        for t in range(15):
            P.op("dve", lambda e, t=t: e.memset(INVC[:, :, t:t + 1], 1.0 / (t + 1)), writes=["invc"])
        for c in range(3):
            w = 2 << c
            P.op("dve", lambda e, c=c, w=w: e.memset(INVC[:, c, w - 1:15], 1.0 / w), writes=["invc"])
        P.dma("sp", lambda e: e.dma_start(out=PPt[:], in_=pp_d), key="cst_pp", writes=["pp"])
        P.dma("sp", lambda e: e.dma_start(out=BSB[:], in_=bsb_d.rearrange("l p g t -> p l g t")), key="cst_bsb", writes=["bsb"])
        P.dma("sp", lambda e: e.dma_start(out=BSS[:], in_=bss_d.rearrange("l p g t -> p l g t")), key="cst_bss", writes=["bss"])
        P.dma("pool", lambda e: e.dma_start(out=WDG[:], in_=wdg_d.rearrange("l p g d -> p l g d")), key="cst2", writes=["wdg"])
        P.dma("sp", lambda e: e.dma_start(out=SB_[0][:, 0:NL * 512].rearrange("p (l g t) -> p l g t", l=NL, g=4),
                                          in_=wsT_d.rearrange("l p g t -> p l g t")), key="in_S0", writes=sblk(0))
        P.op("pool", lambda e: e.affine_select(out=SB_[0][:, 0:NL * 512].rearrange("p (l g t) -> p l g t", l=NL, g=4),
                                               in_=SB_[0][:, 0:NL * 512].rearrange("p (l g t) -> p l g t", l=NL, g=4),
                                               compare_op=ALU.is_ge, fill=0.0, base=0,
                                               pattern=[[0, NL], [0, 4], [1, 128]], channel_multiplier=-1),
             reads=sblk(0), writes=sblk(0))
        P.op("dve", lambda e: e.tensor_copy(WST[:], SB_[0][:, 0:NL * 512].rearrange("p (l g t) -> p l g t", l=NL, g=4)),
             reads=sblk(0), writes=["wst"])
        S1v = SB_[1][0:64, 0:NL * 256].rearrange("p (l g a t) -> p l g a t", l=NL, g=4, a=NS)
        P.dma("sp", lambda e: e.dma_start(out=SB_[1][0:64, 0:NL * 256].rearrange("p (l g f) -> p l g f", l=NL, g=4),
                                          in_=wsS_d.rearrange("l p g f -> p l g f")), key="in_S1", writes=sblk(1))
        P.op("pool", lambda e: e.affine_select(out=S1v, in_=S1v, compare_op=ALU.is_ge, fill=0.0, base=0,
                                               pattern=[[0, NL], [0, 4], [4, NS], [1, TS]], channel_multiplier=-1),
             reads=sblk(1), writes=sblk(1))
        P.op("pool", lambda e: e.affine_select(out=S1v, in_=S1v, compare_op=ALU.is_ge, fill=0.0, base=0,
                                               pattern=[[0, NL], [0, 4], [-4, NS], [0, TS]], channel_multiplier=1),
             reads=sblk(1), writes=sblk(1))
        P.op("dve", lambda e: e.tensor_copy(WSS[:], SB_[1][0:64, 0:NL * 256].rearrange("p (l g f) -> p l g f", l=NL, g=4)),
             reads=sblk(1), writes=["wss"])

        P.tag = "prologue"
        P.dma("sp", lambda e: e.dma_start(out=STG[:], in_=memT_d), key="in_STG", writes=["stg"])
        for j in range(8):
            P.op("act", lambda e, j=j: e.copy(bf8(2, j, 256), STG[:, j, :]), reads=["stg"], writes=[bf8r(2, j)])
        for l in range(NL):
            for half in range(2):
                wv, wr = load_w(wsrc(w_mk[l], half * 512, 512), 8, 512)
                for jj in range(4):
                    j = half * 4 + jj
                    pb = next_ps()
                    mm(PS[pb][:, 0:256], [(wv[:, kc, jj * 128:(jj + 1) * 128], bf8(2, kc, 256)) for kc in range(8)],
                       reads=[wr] + sblk(2), psreg=f"ps{pb}")
                    P.op("act", lambda e, pb=pb, j=j: e.copy(STG[:, j, :], PS[pb][:, 0:256]),
                         reads=[f"ps{pb}"], writes=["stg"])
                    P.op("dve", lambda e, j=j, l=l: e.tensor_copy(KT[:, l, j, :], STG[:, j, :]),
                         reads=["stg"], writes=["kt"])
            P.dma("sp", lambda e, l=l: e.dma_start(out=o_mk[l], in_=STG[:]), key="o_STG", reads=["stg"])
            for half in range(2):
                wv, wr = load_w(wsrc(w_mv[l], half * 512, 512), 8, 512)
                for mc in range(2):
                    pb = next_ps()
                    mm(PS[pb][:, 0:512], [(bf8(2, kc, 256)[:, mc * 128:(mc + 1) * 128], wv[:, kc, :]) for kc in range(8)],
                       reads=[wr] + sblk(2), psreg=f"ps{pb}")
                    P.op("act", lambda e, pb=pb, mc=mc, half=half: e.copy(SB_[3 + mc][:, half * 512:(half + 1) * 512], PS[pb][:, 0:512]),
                         reads=[f"ps{pb}"], writes=sblk(3 + mc))
                    P.op("dve", lambda e, mc=mc, half=half, l=l: e.tensor_copy(VT[:, l, mc, half * 512:(half + 1) * 512], SB_[3 + mc][:, half * 512:(half + 1) * 512]),
                         reads=sblk(3 + mc), writes=["vt"])
            for mc in range(2):
                P.dma("sp", lambda e, l=l, mc=mc: e.dma_start(out=o_mv[l, :, mc, :], in_=SB_[3 + mc][:, 0:1024]),
                      key=f"o_S{3 + mc}", reads=sblk(3 + mc))

        def preswitch_ln():
            P.op("act", lambda e: e.activation(DUMMY[:, 0:1], ONES[:, 2, 0:1], AF.Ln), reads=["ones"], writes=["dummy"])

        def rmsnorm(tc, gname, l, out_bf=True, out_fn=None):
            N = tc.N
            preswitch_ln()
            P.op("act", lambda e: e.activation(SQ[:, 0:4, 0:N], XT[:, 0:4, 0:N], AF.Square), reads=R("x", 8)[0:4], writes=["sqa"])
            P.op("dve", lambda e: e.tensor_tensor(out=SQ[:, 4:8, 0:N], in0=XT[:, 4:8, 0:N], in1=XT[:, 4:8, 0:N], op=ALU.mult),
                 reads=R("x", 8)[4:8], writes=["sqb"])
            pb = next_ps()
            mm(PS[pb][:, 0:N], [(ONES[:, 0, :], SQ[:, c, 0:N]) for c in range(8)], reads=["sqa", "sqb", "ones"], psreg=f"ps{pb}")
            P.op("act", lambda e: e.activation(STAT[0][:, 0:N], PS[pb][:, 0:N], AF.Ln, bias=EPS), reads=[f"ps{pb}"], writes=["stat0"])
            P.op("act", lambda e: e.activation(STAT[0][:, 0:N], STAT[0][:, 0:N], AF.Exp, scale=-0.5), reads=["stat0"], writes=["stat0"])
            for c in range(8):
                if l is None:
                    g = PPt[:, NL * PPL + c: NL * PPL + c + 1]
                else:
                    g = ppc(l, gname, c)
                if out_fn is None:
                    P.op("dve", lambda e, c=c, g=g: e.scalar_tensor_tensor(out=HB[:, c, 0:N], in0=XT[:, c, 0:N], scalar=g,
                                                                           in1=STAT[0][:, 0:N], op0=ALU.mult, op1=ALU.mult),
                         reads=[f"x{c}", "stat0", "pp"], writes=[f"hb{c}"])
                else:
                    oap, oreg = out_fn(c)
                    P.op("dve", lambda e, c=c, g=g, oap=oap: e.scalar_tensor_tensor(out=oap, in0=XT[:, c, 0:N], scalar=g,
                                                                                    in1=STAT[0][:, 0:N], op0=ALU.mult, op1=ALU.mult),
                         reads=[f"x{c}", "stat0", "pp"], writes=[oreg])

        def ln_stats(tc, blk):
            N = tc.N
            preswitch_ln()
            P.op("act", lambda e: e.copy(SQ[:, 0:4, 0:N], quarters(blk, 0, N)), reads=sblk(blk), writes=["sqa"])
            P.op("act", lambda e: e.activation(SQ[:, 4:8, 0:N], quarters(blk, 0, N), AF.Square), reads=sblk(blk), writes=["sqb"])
            pm = next_ps()
            mm(PS[pm][:, 0:N], [(ONES[:, 1, :], SQ[:, c, 0:N]) for c in range(4)], reads=["sqa", "ones"], psreg=f"ps{pm}")
            pq = next_ps()
            mm(PS[pq][:, 0:N], [(ONES[:, 1, :], SQ[:, 4 + c, 0:N]) for c in range(4)], reads=["sqb", "ones"], psreg=f"ps{pq}")
            P.op("act", lambda e: e.activation(STAT[1][:, 0:N], PS[pm][:, 0:N], AF.Square), reads=[f"ps{pm}"], writes=["stat1"])
            P.op("dve", lambda e: e.tensor_tensor(out=STAT[1][:, 0:N], in0=PS[pq][:, 0:N], in1=STAT[1][:, 0:N], op=ALU.subtract),
                 reads=[f"ps{pq}", "stat1"], writes=["stat1"])
            P.op("act", lambda e: e.activation(STAT[1][:, 0:N], STAT[1][:, 0:N], AF.Ln, bias=EPS), reads=["stat1"], writes=["stat1"])
            P.op("act", lambda e: e.activation(STAT[1][:, 0:N], STAT[1][:, 0:N], AF.Exp, scale=-0.5), reads=["stat1"], writes=["stat1"])
            return pm

        def hist_in(tc, l, blk, H, hist_t, hist_nm, st_d_ap, nch=4):
            for sg in tc.segs:
                if sg.kind == "p":
                    P.op("act", lambda e: e.copy(quarters(blk, 0, H, nch), hist_t[:, l, :, :]),
                         reads=[f"{hist_nm}{l}"], writes=sblk(blk)[0:nch])
                else:
                    SH = sg.S * H
                    stg = STS[:, 0:nch * SH].rearrange("p (c x) -> p c x", c=nch)
                    P.dma("sp", lambda e, sg=sg, stg=stg: e.dma_start(out=stg, in_=st_d_ap[l, :, :, sg.s0:sg.s0 + sg.S, :].rearrange("p c s h -> p c (s h)")),
                          key="in_STS", writes=["sts"])
                    for c in range(nch):
                        P.op("act", lambda e, c=c, sg=sg, stg=stg: e.copy(ext(sg, blk, c, H, 0, H), stg[:, c, :].rearrange("p (s h) -> p s h", s=sg.S)),
                             reads=["sts"], writes=[sq_(blk, c)])

        def hist_out(tc, l, blk, H, hist_t, hist_nm, o_p, o_s, nch=4):
            for sg in tc.segs:
                T = sg.T
                if sg.kind == "p":
                    P.op("act", lambda e, T=T: e.copy(hist_t[:, l, :, :], quarters(blk, T, T + H, nch)),
                         reads=sblk(blk)[0:nch], writes=[f"{hist_nm}{l}"])
                    if sg.last:
                        P.dma("sp", lambda e: e.dma_start(out=o_p[l], in_=hist_t[:, l, :, :]), key=f"o_{hist_nm}", reads=[f"{hist_nm}{l}"])
                else:
                    SH = sg.S * H
                    stg = STS[:, 0:nch * SH].rearrange("p (c x) -> p c x", c=nch)
                    for c in range(nch):
                        P.op("act", lambda e, c=c, sg=sg, T=T, stg=stg: e.copy(stg[:, c, :].rearrange("p (s h) -> p s h", s=sg.S), ext(sg, blk, c, H, T, T + H)),
                             reads=[sq_(blk, c)], writes=["sts"])
                    P.dma("sp", lambda e, sg=sg, stg=stg: e.dma_start(out=o_s[l, :, :, sg.s0:sg.s0 + sg.S, :].rearrange("p c s h -> p c (s h)"), in_=stg),
                          key="o_STS", reads=["sts"])

        def layer(tc, l):
            N = tc.N
            segs = tc.segs

            def zmm(wv, wr, jj):
                pb = next_ps()
                mm(PS[pb][:, 0:N], [(wv[:, kc, jj * 128:(jj + 1) * 128], HB[:, kc, 0:N]) for kc in range(8)],
                   reads=[wr] + R("hb", 8), psreg=f"ps{pb}")
                return pb

            def win_unit(u):
                return load_w(wsrc(w_in[l], u * 512, 512), 8, 512)

            def conv_taps(src_blk, dst_blk, H, wname, bname, ntap):
                for k in range(ntap):
                    for c in range(4):
                        wk = ppc(l, wname, c * ntap + k)
                        for sg in segs:
                            T = sg.T
                            if k == 0:
                                P.op("dve", lambda e, c=c, wk=wk, sg=sg, T=T: e.tensor_scalar(out=fseg(sg, flat(tc, dst_blk, c)), in0=ext(sg, src_blk, c, H, 0, T),
                                                                                                 scalar1=wk, scalar2=ppc(l, bname, c), op0=ALU.mult, op1=ALU.add),
                                     reads=[sq_(src_blk, c), "pp"], writes=[sq_(dst_blk, c)])
                            else:
                                P.op("dve", lambda e, c=c, wk=wk, k=k, sg=sg, T=T: e.scalar_tensor_tensor(out=fseg(sg, flat(tc, dst_blk, c)), in0=ext(sg, src_blk, c, H, k, k + T),
                                                                                                             scalar=wk, in1=fseg(sg, flat(tc, dst_blk, c)), op0=ALU.mult, op1=ALU.add),
                                     reads=[sq_(src_blk, c), sq_(dst_blk, c), "pp"], writes=[sq_(dst_blk, c)])

            def ln_apply(blk, pm):
                P.op("dve", lambda e, pm=pm: e.tensor_tensor(out=quarters(blk, 0, N), in0=quarters(blk, 0, N),
                                                            in1=PS[pm][:, 0:N].unsqueeze(1).broadcast_to([128, 4, N]), op=ALU.subtract),
                     reads=sblk(blk) + [f"ps{pm}"], writes=sblk(blk))
                P.op("dve", lambda e: e.tensor_tensor(out=quarters(blk, 0, N), in0=quarters(blk, 0, N),
                                                      in1=STAT[1][:, 0:N].unsqueeze(1).broadcast_to([128, 4, N]), op=ALU.mult),
                     reads=sblk(blk) + ["stat1"], writes=sblk(blk))

            P.tag = "norm1"
            rmsnorm(tc, "g_mix", l)

            P.tag = "A"
            wv, wr = win_unit(1)
            for c in range(4):
                pb = zmm(wv, wr, c)
                P.op("act", lambda e, pb=pb, c=c: e.activation(flat(tc, 0, c), PS[pb][:, 0:N], AF.Sigmoid),
                     reads=[f"ps{pb}"], writes=[sq_(0, c)])
            hist_in(tc, l, 1, 30, HA, "ha", st_a)
            wv, wr = win_unit(0)
            for c in range(4):
                pb = zmm(wv, wr, c)
                for sg in segs:
                    P.op("dve", lambda e, pb=pb, c=c, sg=sg: e.tensor_tensor(out=ext(sg, 1, c, 30, 30, 30 + sg.T), in0=fseg(sg, PS[pb][:, 0:N]),
                                                                             in1=fseg(sg, flat(tc, 0, c)), op=ALU.mult),
                         reads=[f"ps{pb}", sq_(0, c)], writes=[sq_(1, c)])
            hist_out(tc, l, 1, 30, HA, "ha", o_ap, o_as)
            ELA = tc.extlen(30)
            for cb_ in (0, 2):
                prep = {}
                for c in (cb_, cb_ + 1):
                    ub = SBb[0][:, c * 2 * BLK: c * 2 * BLK + ELA]
                    P.op("act", lambda e, c=c, ub=ub: e.copy(ub, SB_[1][:, c * BLK: c * BLK + ELA]),
                         reads=[sq_(1, c)], writes=[sq_(0, c)])
                    sl_ = cnt["slot"] % NSLOT
                    cnt["slot"] += 1
                    dw = WS[sl_][:, 0:31 * 128].rearrange("p (k j) -> p k j", k=31)
                    wofs = l * PPL + PPO["conv_a_w"] + c * 31
                    P.op("pool", lambda e, dw=dw, wofs=wofs: e.tensor_tensor(out=dw, in0=IDENT[:, :].unsqueeze(1).broadcast_to([128, 31, 128]),
                                                                             in1=PPt[:, wofs:wofs + 31].unsqueeze(2).broadcast_to([128, 31, 128]),
                                                                             op=ALU.mult),
                         reads=["ident", "pp"], writes=[f"ws{sl_}"])
                    prep[c] = (ub, dw, sl_)
                pbs = {}
                for c in (cb_, cb_ + 1):
                    ub, dw, sl_ = prep[c]
                    pb = next_ps()
                    pbs[c] = pb
                    for sg in segs:
                        LA = 30 + sg.T
                        ubs = ub[:, sg.ebase(30):sg.ebase(30) + sg.S * LA]
                        if sg.S == 1:
                            ubv = lambda k, ubs=ubs, sg=sg: ubs[:, k:k + sg.T]
                        else:
                            ubv = lambda k, ubs=ubs, sg=sg: ubs.rearrange("p (s j) -> p s j", s=sg.S)[:, :, k:k + sg.T]
                        mm(fseg(sg, PS[pb][:, 0:N]), [(dw[:, k, :], ubv(k)) for k in range(31)], reads=[f"ws{sl_}", sq_(0, c)], psreg=f"ps{pb}")
                for c in (cb_, cb_ + 1):
                    pb = pbs[c]
                    P.op("act", lambda e, pb=pb, c=c: e.activation(flat(tc, 2, c), PS[pb][:, 0:N], AF.Identity, bias=ppc(l, "conv_a_b", c)),
                         reads=[f"ps{pb}", "pp"], writes=[sq_(2, c)])
            pm = ln_stats(tc, 2)
            ln_apply(2, pm)
            for c in range(4):
                P.op("act", lambda e, c=c: e.activation(YIN[0][:, c, 0:N], flat(tc, 2, c), AF.Silu,
                                                        bias=ppc(l, "ln_a_b", c), scale=ppc(l, "ln_a_g", c)),
                     reads=[sq_(2, c), "pp"], writes=[f"yin0_{c}"])

            P.tag = "B"
            wv, wr = win_unit(3)
            for c in range(4):
                pb = zmm(wv, wr, c)
                P.op("act", lambda e, pb=pb, c=c: e.copy(flat(tc, 3, c), PS[pb][:, 0:N]), reads=[f"ps{pb}"], writes=[sq_(3, c)])
            hist_in(tc, l, 4, 2, HBh, "hb_", st_b)
            wv, wr = win_unit(4)
            for c in range(4):
                pb = zmm(wv, wr, c)
                for sg in segs:
                    P.op("dve", lambda e, pb=pb, c=c, sg=sg: e.tensor_tensor(out=ext(sg, 4, c, 2, 2, 2 + sg.T), in0=fseg(sg, PS[pb][:, 0:N]),
                                                                             in1=fseg(sg, flat(tc, 3, c)), op=ALU.mult),
                         reads=[f"ps{pb}", sq_(3, c)], writes=[sq_(4, c)])
            hist_out(tc, l, 4, 2, HBh, "hb_", o_bp, o_bs)
            conv_taps(4, 0, 2, "conv_b_w", "conv_b_b", 3)
            wv, wr = win_unit(2)
            for c in range(4):
                pb = zmm(wv, wr, c)
                P.op("dve", lambda e, pb=pb, c=c: e.tensor_tensor(out=YIN[1][:, c, 0:N], in0=PS[pb][:, 0:N], in1=flat(tc, 0, c), op=ALU.mult),
                     reads=[f"ps{pb}", sq_(0, c)], writes=[f"yin1_{c}"])

            P.tag = "C"
            wv, wr = win_unit(5)
            for c in range(4):
                pb = zmm(wv, wr, c)
                P.op("act", lambda e, pb=pb, c=c: e.activation(flat(tc, 1, c), PS[pb][:, 0:N], AF.Gelu_apprx_tanh),
                     reads=[f"ps{pb}"], writes=[sq_(1, c)])
            wv, wr = win_unit(6)
            for c in range(4):
                pb = zmm(wv, wr, c)
                P.op("act", lambda e, pb=pb, c=c: e.activation(flat(tc, 2, c), PS[pb][:, 0:N], AF.Gelu_apprx_tanh),
                     reads=[f"ps{pb}"], writes=[sq_(2, c)])
            pm = ln_stats(tc, 2)
            ln_apply(2, pm)
            for c in range(4):
                P.op("act", lambda e, c=c: e.activation(flat(tc, 2, c), flat(tc, 2, c), AF.Identity,
                                                        bias=ppc(l, "ln_c_b", c), scale=ppc(l, "ln_c_g", c)),
                     reads=[sq_(2, c), "pp"], writes=[sq_(2, c)])
            for sg in segs:
                if sg.kind == "p":
                    if sg.last:
                        for c in range(4):
                            P.dma("sp", lambda e, c=c, sg=sg: e.dma_start(out=o_vp[l, :, c, :], in_=flat(tc, 2, c)[:, sg.c0 + sg.T - 128:sg.c0 + sg.T]),
                                  key="o_S2", reads=[sq_(2, c)])
                else:
                    for c in range(4):
                        P.dma("sp", lambda e, c=c, sg=sg: e.dma_start(out=o_vs[l, :, c, sg.s0 * TS:(sg.s0 + sg.S) * TS],
                                                                       in_=flat(tc, 2, c)[:, sg.c0:sg.c0 + sg.n]),
                              key="o_S2", reads=[sq_(2, c)])
            pblocks = []
            for sg in segs:
                if sg.kind == "p":
                    for i in range(sg.T // 128):
                        pblocks.append((sg.c0 + i * 128, 128, "p"))
                else:
                    pblocks.append((sg.c0, sg.n, "s"))
            for pi, (pc0, PW, kd) in enumerate(pblocks):
                pb = next_ps()
                for c in range(4):
                    P.op("pe", lambda e, pb=pb, pc0=pc0, PW=PW, c=c: e.transpose(PS[pb][0:PW, c * 128:(c + 1) * 128],
                                                                                 flat(tc, 2, c)[:, pc0:pc0 + PW], IDENT[:]),
                         reads=[sq_(2, c), "ident"], writes=[f"ps{pb}"])
                P.op("act", lambda e, pb=pb, pi=pi, PW=PW: e.copy(VNT[0:PW, pi, :, :], PS[pb][0:PW, 0:512].rearrange("p (c d) -> p c d", c=4)),
                     reads=[f"ps{pb}"], writes=["vnt"])
            for c in range(4):
                pb = next_ps()
                g = next_gs()
                for pi, (pc0, PW, kd) in enumerate(pblocks):
                    rhs = WST[:, l, c, :] if kd == "p" else WSS[0:PW, l, c, 0:PW]
                    mm(PS[pb][:, pc0:pc0 + PW], [(VNT[0:PW, pi, c, :], rhs)], reads=["vnt", "wst", "wss"], psreg=f"ps{pb}")
                for pi, (pc0, PW, kd) in enumerate(pblocks):
                    bias = BSB[:, l, c, :] if kd == "p" else BSS[:, l, c, 0:PW]
                    P.op("dve", lambda e, pb=pb, pc0=pc0, PW=PW, g=g, bias=bias: e.tensor_tensor(out=GS[g][:, pc0:pc0 + PW], in0=PS[pb][:, pc0:pc0 + PW],
                                                                                               in1=bias, op=ALU.add),
                         reads=[f"ps{pb}", "bsb", "bss"], writes=[f"gs{g}"])
                P.op("dve", lambda e, c=c, g=g: e.tensor_tensor(out=YIN[2][:, c, 0:N], in0=GS[g][:, 0:N], in1=flat(tc, 1, c), op=ALU.mult),
                     reads=[f"gs{g}", sq_(1, c)], writes=[f"yin2_{c}"])

            P.tag = "D"
            hist_in(tc, l, 3, 15, HD, "hd", st_d)
            wv, wr = win_unit(7)
            for c in range(4):
                pb = zmm(wv, wr, c)
                for sg in segs:
                    P.op("act", lambda e, pb=pb, c=c, sg=sg: e.copy(ext(sg, 3, c, 15, 15, 15 + sg.T), fseg(sg, PS[pb][:, 0:N])),
                         reads=[f"ps{pb}"], writes=[sq_(3, c)])
            hist_out(tc, l, 3, 15, HD, "hd", o_dp, o_ds)
            for c in range(4):
                w = 2 << c
                for sg in segs:
                    L = 15 + sg.T
                    src_blk = 3
                    for k in range(c + 1):
                        d = 1 << k
                        lo = (2 << k) - 1
                        dst_blk = 4 if (k % 2 == 0) else 0
                        P.op("dve", lambda e, c=c, sg=sg, L=L, src_blk=src_blk, dst_blk=dst_blk, d=d, lo=lo:
                             e.tensor_tensor(out=ext(sg, dst_blk, c, 15, lo, L), in0=ext(sg, src_blk, c, 15, lo, L),
                                             in1=ext(sg, src_blk, c, 15, lo - d, L - d), op=ALU.add),
                             reads=[sq_(src_blk, c)], writes=[sq_(dst_blk, c)])
                        src_blk = dst_blk
                    P.op("dve", lambda e, c=c, sg=sg, L=L, src_blk=src_blk, w=w: e.scalar_tensor_tensor(out=fseg(sg, SQ[:, c, 0:N]), in0=ext(sg, src_blk, c, 15, 15, L),
                                                                                                     scalar=1.0 / w, in1=ext(sg, 3, c, 15, 15, L),
                                                                                                     op0=ALU.mult, op1=ALU.subtract),
                         reads=[sq_(src_blk, c), sq_(3, c)], writes=["sqa"])
                    if sg.kind == "p" and sg.t0 == 0:
                        g = next_gs()
                        P.op("dve", lambda e, c=c, sg=sg, src_blk=src_blk, g=g: e.tensor_tensor(out=GS[g][:, 0:15], in0=ext(sg, src_blk, c, 15, 15, 30),
                                                                                             in1=INVC[:, c, :], op=ALU.mult),
                             reads=[sq_(src_blk, c), "invc"], writes=[f"gs{g}"])
                        P.op("dve", lambda e, c=c, sg=sg, g=g: e.tensor_tensor(out=SQ[:, c, sg.c0:sg.c0 + 15], in0=GS[g][:, 0:15], in1=ext(sg, 3, c, 15, 15, 30), op=ALU.subtract),
                             reads=[f"gs{g}", sq_(3, c)], writes=["sqa"])
            for c in range(4):
                pb = next_ps()
                mm(PS[pb][:, 0:N], [(WDG[:, l, c, :], SQ[:, c, 0:N])], reads=["sqa", "wdg"], psreg=f"ps{pb}")
                P.op("act", lambda e, pb=pb, c=c: e.activation(YIN[3][:, c, 0:N], PS[pb][:, 0:N], AF.Identity, scale=ppc(l, "scale_d", c)),
                     reads=[f"ps{pb}", "pp"], writes=[f"yin3_{c}"])

            P.tag = "merge"
            for i, wo in enumerate((w_a_out, w_b_out, w_c_out, w_d_out)):
                wv, wr = load_w(wsrc(wo[l], 0, 1024), 4, 1024)
                for half in range(2):
                    gv, gr = win_unit(8 + 2 * i + half)
                    for oo in range(4):
                        o = half * 4 + oo
                        acc = flat(tc, 3 + o // 4, o % 4)
                        accr = sq_(3 + o // 4, o % 4)
                        py = next_ps()
                        mm(PS[py][:, 0:N], [(wv[:, kc, o * 128:(o + 1) * 128], YIN[i][:, kc, 0:N]) for kc in range(4)],
                           reads=[wr] + [f"yin{i}_{kc}" for kc in range(4)], psreg=f"ps{py}")
                        pg = zmm(gv, gr, oo)
                        g = next_gs()
                        P.op("act", lambda e, pg=pg, g=g: e.activation(GS[g][:, 0:N], PS[pg][:, 0:N], AF.Sigmoid),
                             reads=[f"ps{pg}"], writes=[f"gs{g}"])
                        if i == 0:
                            P.op("dve", lambda e, py=py, g=g, acc=acc: e.tensor_tensor(out=acc, in0=PS[py][:, 0:N], in1=GS[g][:, 0:N], op=ALU.mult),
                                 reads=[f"ps{py}", f"gs{g}"], writes=[accr])
                        else:
                            P.op("dve", lambda e, py=py, g=g: e.tensor_tensor(out=GS[g][:, 0:N], in0=PS[py][:, 0:N], in1=GS[g][:, 0:N], op=ALU.mult),
                                 reads=[f"ps{py}", f"gs{g}"], writes=[f"gs{g}"])
                            if i < 3:
                                P.op("dve", lambda e, g=g, acc=acc: e.tensor_tensor(out=acc, in0=acc, in1=GS[g][:, 0:N], op=ALU.add),
                                     reads=[accr, f"gs{g}"], writes=[accr])
                            else:
                                P.op("dve", lambda e, g=g, acc=acc, o=o: e.tensor_tensor(out=MB[:, o, 0:N], in0=acc, in1=GS[g][:, 0:N], op=ALU.add),
                                     reads=[accr, f"gs{g}"], writes=[f"mb{o}"])

            def proj_residual(w_l, src_fn, src_regs, nk):
                ncol = 512 if nk == 8 else 128
                for u in range(1024 // ncol):
                    wv, wr = load_w(wsrc(w_l, u * ncol, ncol), nk, ncol)
                    for jj in range(ncol // 128):
                        o = u * (ncol // 128) + jj
                        pb = next_ps()
                        mm(PS[pb][:, 0:N], [(wv[:, kc, jj * 128:(jj + 1) * 128], src_fn(kc)) for kc in range(nk)],
                           reads=[wr] + src_regs, psreg=f"ps{pb}")
                        P.op("dve", lambda e, pb=pb, o=o: e.tensor_tensor(out=XT[:, o, 0:N], in0=XT[:, o, 0:N], in1=PS[pb][:, 0:N], op=ALU.add),
                             reads=[f"ps{pb}", f"x{o}"], writes=[f"x{o}"])

            P.tag = "mixout"
            proj_residual(w_mix[l], lambda kc: MB[:, kc, 0:N], R("mb", 8), 8)

            P.tag = "attn"
            rmsnorm(tc, "g_xattn", l)
            for half in range(2):
                wv, wr = load_w(wsrc(w_q[l], half * 512, 512), 8, 512)
                for jj in range(4):
                    j = half * 4 + jj
                    pb = zmm(wv, wr, jj)
                    P.op("act", lambda e, pb=pb, j=j: e.activation(bf8(0, j, N), PS[pb][:, 0:N], AF.Identity, scale=1.0 / 16),
                         reads=[f"ps{pb}"], writes=[bf8r(0, j)])
            for sg in segs:
                c0, n = sg.c0, sg.n
                if sg.kind == "p":
                    for h in range(4):
                        for mc in range(2):
                            pb = next_ps()
                            mm(PS[pb][:, 0:n], [(KT[:, l, 2 * h + dc, mc * 128:(mc + 1) * 128], bf8(0, 2 * h + dc, N)[:, c0:c0 + n]) for dc in range(2)],
                               reads=["kt", bf8r(0, 2 * h)], psreg=f"ps{pb}")
                            P.op("act", lambda e, pb=pb, h=h, mc=mc, c0=c0, n=n: e.activation(bf8(1, 2 * h + mc, N)[:, c0:c0 + n], PS[pb][:, 0:n], AF.Exp),
                                 reads=[f"ps{pb}"], writes=[bf8r(1, 2 * h + mc)])
                        pd = next_ps()
                        mm(PS[pd][:, 0:n], [(ONES[:, 2, :], bf8(1, 2 * h + mc, N)[:, c0:c0 + n]) for mc in range(2)], reads=["ones", bf8r(1, 2 * h)], psreg=f"ps{pd}")
                        g = next_gs()
                        P.op("dve", lambda e, pd=pd, g=g, n=n: e.reciprocal(GS[g][:, 0:n], PS[pd][:, 0:n]), reads=[f"ps{pd}"], writes=[f"gs{g}"])
                        for dc in range(2):
                            j = 2 * h + dc
                            pb = next_ps()
                            mm(PS[pb][:, 0:n], [(VT[:, l, mc, j * 128:(j + 1) * 128], bf8(1, 2 * h + mc, N)[:, c0:c0 + n]) for mc in range(2)],
                               reads=["vt", bf8r(1, 2 * h)], psreg=f"ps{pb}")
                            P.op("dve", lambda e, pb=pb, j=j, g=g, c0=c0, n=n: e.tensor_tensor(out=bf8(2, j, N)[:, c0:c0 + n], in0=PS[pb][:, 0:n], in1=GS[g][:, 0:n], op=ALU.mult),
                                 reads=[f"ps{pb}", f"gs{g}"], writes=[bf8r(2, j)])
                else:
                    ETS = STAT[1].bitcast(BF16)[:, 0:8 * n].rearrange("p (a x) -> p a x", a=8)
                    psc = next_ps()
                    pso = next_ps()
                    kvs = {}

                    def issue_kv(si, sg=sg):
                        if si >= sg.S or si in kvs:
                            return
                        s_ = sg.s0 + si
                        sl_ = cnt["slot"] % NSLOT
                        cnt["slot"] += 1
                        kv_ = WS[sl_][:, 0:2048].rearrange("p (j m) -> p j m", j=8)
                        vv_ = WS[sl_][:, 2048:4096].rearrange("p (c f) -> p c f", c=2)
                        P.dma("pool", lambda e, s_=s_, kv_=kv_: e.dma_start(out=kv_, in_=ckT[l, s_]), key=f"w{sl_}", writes=[f"ws{sl_}"])
                        P.dma("pool", lambda e, s_=s_, vv_=vv_: e.dma_start(out=vv_, in_=cvN[l, s_]), key=f"w{sl_}", writes=[f"ws{sl_}"])
                        kvs[si] = (kv_, vv_, f"ws{sl_}")

                    issue_kv(0)
                    issue_kv(1)
                    issue_kv(2)
                    for si in range(sg.S):
                        issue_kv(si + 3)
                        kv, vv, kvr = kvs[si]
                        for h in range(4):
                            for mc in range(2):
                                col = (h * 2 + mc) * n + si * TS
                                mm(PS[psc][:, col:col + TS],
                                   [(kv[:, 2 * h + dc, mc * 128:(mc + 1) * 128], bf8(0, 2 * h + dc, N)[:, c0 + si * TS:c0 + (si + 1) * TS]) for dc in range(2)],
                                   reads=[kvr, bf8r(0, 2 * h)], psreg=f"ps{psc}")
                        P.op("act", lambda e, si=si, n=n, ETS=ETS, psc=psc: e.activation(ETS[:, :, si * TS:(si + 1) * TS],
                                                                                 PS[psc][:, 0:8 * n].rearrange("p (a x) -> p a x", a=8)[:, :, si * TS:(si + 1) * TS], AF.Exp),
                             reads=[f"ps{psc}"], writes=["stat1"])
                        for j in range(8):
                            h = j // 2
                            col = j * n + si * TS
                            mm(PS[pso][:, col:col + TS],
                               [(vv[:, mc, j * 128:(j + 1) * 128], ETS[:, 2 * h + mc, si * TS:(si + 1) * TS]) for mc in range(2)],
                               reads=[kvr, "stat1"], psreg=f"ps{pso}")
                    pd = next_ps()
                    for h in range(4):
                        mm(PS[pd][:, h * n:(h + 1) * n], [(ONES[:, 2, :], ETS[:, 2 * h + mc, :]) for mc in range(2)],
                           reads=["ones", "stat1"], psreg=f"ps{pd}")
                    g = next_gs()
                    P.op("dve", lambda e, pd=pd, g=g, n=n: e.reciprocal(GS[g][:, 0:4 * n], PS[pd][:, 0:4 * n]), reads=[f"ps{pd}"], writes=[f"gs{g}"])
                    for j in range(8):
                        h = j // 2
                        P.op("dve", lambda e, j=j, h=h, g=g, c0=c0, n=n, pso=pso: e.tensor_tensor(out=bf8(2, j, N)[:, c0:c0 + n], in0=PS[pso][:, j * n:(j + 1) * n],
                                                                                        in1=GS[g][:, h * n:(h + 1) * n], op=ALU.mult),
                             reads=[f"ps{pso}", f"gs{g}"], writes=[bf8r(2, j)])
            P.tag = "xo"
            proj_residual(w_xo[l], lambda kc: bf8(2, kc, N), sblk(2), 8)

            P.tag = "ffn_up"
            rmsnorm(tc, "g_ffn", l)
            for sg in segs:
                if sg.kind == "s":
                    P.dma("sp", lambda e, sg=sg: e.dma_start(out=HFS[:, sg.s0 // 8], in_=st_f[l, :, sg.s0 // 8]),
                          key="in_HFS", writes=["hfs"])

            def fin_ap(j):
                return bf8(2 + j // 8, j % 8, N), bf8r(2 + j // 8, j % 8)

            for u in range(11):
                s = cnt["slot"] % NSLOT
                cnt["slot"] += 1
                wv = WS[s][:, 0:4096].rearrange("p (k a n) -> p k a n", k=8, a=2)
                src = w_up[l].rearrange("(k p) (a n) -> p k a n", p=128, a=2)[:, :, :, u * 256:(u + 1) * 256]
                for a_ in range(2):
                    P.dma("pool", lambda e, wv=wv, src=src, a_=a_: e.dma_start(out=wv[:, :, a_, :], in_=src[:, :, a_, :]),
                          key=f"w{s}", writes=[f"ws{s}"])
                wr = f"ws{s}"
                conv_regs = {}
                for jj in range(2):
                    for a in range(2):
                        j = a * 22 + 2 * u + jj
                        pb = next_ps()
                        mm(PS[pb][:, 0:N], [(wv[:, kc, a, jj * 128:(jj + 1) * 128], HB[:, kc, 0:N]) for kc in range(8)],
                           reads=[wr] + R("hb", 8), psreg=f"ps{pb}")
                        r = cnt["scr"] % 4
                        cnt["scr"] += 1
                        P.op("act", lambda e, r=r, pb=pb, j=j: e.activation(flat(tc, 1, r), PS[pb][:, 0:N], AF.Identity,
                                                                            bias=ppc(l, "ffn_conv_b", j), scale=ppc(l, "ffn_conv_w", j * 3 + 2)),
                             reads=[f"ps{pb}", "pp"], writes=[sq_(1, r)])
                        for sg in segs:
                            T = sg.T
                            if sg.kind == "p":
                                hsrc, hreg = HF[:, l, j, :], f"hf{l}"
                            else:
                                hsrc, hreg = HFS[:, sg.s0 // 8, j, :, :], "hfs"
                            P.op("act", lambda e, r=r, hsrc=hsrc, sg=sg: e.copy(ext(sg, 0, r, 2, 0, 2), hsrc), reads=[hreg], writes=[sq_(0, r)])
                            P.op("act", lambda e, r=r, pb=pb, sg=sg, T=T: e.copy(ext(sg, 0, r, 2, 2, 2 + T), fseg(sg, PS[pb][:, 0:N])), reads=[f"ps{pb}"], writes=[sq_(0, r)])
                            P.op("act", lambda e, r=r, hsrc=hsrc, sg=sg, T=T: e.copy(hsrc, ext(sg, 0, r, 2, T, T + 2)), reads=[sq_(0, r)], writes=[hreg])
                        for k in range(2):
                            wk = ppc(l, "ffn_conv_w", j * 3 + k)
                            for sg in segs:
                                T = sg.T
                                P.op("dve", lambda e, r=r, wk=wk, k=k, sg=sg, T=T: e.scalar_tensor_tensor(out=fseg(sg, flat(tc, 1, r)), in0=ext(sg, 0, r, 2, k, k + T),
                                                                                                             scalar=wk, in1=fseg(sg, flat(tc, 1, r)),
                                                                                                             op0=ALU.mult, op1=ALU.add),
                                     reads=[sq_(0, r), sq_(1, r), "pp"], writes=[sq_(1, r)])
                        conv_regs[(jj, a)] = r
                    ra, rb = conv_regs[(jj, 0)], conv_regs[(jj, 1)]
                    g = next_gs()
                    P.op("act", lambda e, ra=ra, g=g: e.activation(GS[g][:, 0:N], flat(tc, 1, ra), AF.Silu), reads=[sq_(1, ra)], writes=[f"gs{g}"])
                    fa, fr = fin_ap(2 * u + jj)
                    P.op("dve", lambda e, rb=rb, g=g, fa=fa: e.tensor_tensor(out=fa, in0=GS[g][:, 0:N], in1=flat(tc, 1, rb), op=ALU.mult),
                         reads=[f"gs{g}", sq_(1, rb)], writes=[fr])
            for sg in segs:
                if sg.kind == "p":
                    if sg.last:
                        P.dma("sp", lambda e: e.dma_start(out=o_fp[l], in_=HF[:, l, :, :]), key="o_hf", reads=[f"hf{l}"])
                else:
                    P.dma("sp", lambda e, sg=sg: e.dma_start(out=o_fs[l, :, sg.s0 // 8], in_=HFS[:, sg.s0 // 8]),
                          key="o_HFS", reads=["hfs"])
            P.tag = "down"
            banks = [next_ps() for _ in range(8)]
            for ug in range(6):
                k0 = ug * 4
                nk_ = min(4, 22 - k0)
                src = w_down[l][k0 * 128:(k0 + nk_) * 128, :].rearrange("(k p) n -> p k n", p=128)
                wv, wr = load_w(src, nk_, 1024)
                for kk in range(nk_):
                    kc = k0 + kk
                    fa, fr = fin_ap(kc)

                    def fn(e, wv=wv, kk=kk, fa=fa, kc=kc):
                        ins = None
                        for o in range(8):
                            ins = e.matmul(PS[banks[o]][:, 0:N], wv[:, kk, o * 128:(o + 1) * 128], fa, start=(kc == 0), stop=(kc == 21))
                        return ins
                    P.op("pe", fn, reads=[wr, fr], writes=[f"ps{b}" for b in banks])
            for o in range(8):
                pb = banks[o]
                P.op("dve", lambda e, pb=pb, o=o: e.tensor_tensor(out=XT[:, o, 0:N], in0=XT[:, o, 0:N], in1=PS[pb][:, 0:N], op=ALU.add),
                     reads=[f"ps{pb}", f"x{o}"], writes=[f"x{o}"])

        tiles = [TC(i, [Seg("p", 1, 512, 0, t0=512 * i)]) for i in range(3)]
        tiles.append(TC(3, [Seg("p", 1, 256, 0, t0=1536), Seg("s", 8, TS, 256, s0=0, prevT=256)]))
        tiles.append(TC(4, [Seg("p", 1, 256, 0, t0=1792, last=True), Seg("s", 8, TS, 256, s0=8, prevT=256)]))
        if tile_sel is not None:
            tiles = [tiles[i] for i in tile_sel]
        for tc in tiles:
            N = tc.N
            for sg in tc.segs:
                if sg.kind == "p":
                    src = xpT[:, :, sg.t0:sg.t0 + sg.T]
                else:
                    src = xsT[:, :, sg.s0 * TS:(sg.s0 + sg.S) * TS]
                P.dma("sp", lambda e, src=src, sg=sg: e.dma_start(out=XT[:, :, sg.c0:sg.c0 + sg.n], in_=src), key="in_x", writes=R("x", 8))
            for l in range(nlayers):
                layer(tc, l)
            P.tag = "final"
            rmsnorm(tc, None, None, out_fn=lambda c: (flat(tc, c // 4, c % 4), sq_(c // 4, c % 4)))
            for c in range(8):
                for sg in tc.segs:
                    if sg.kind == "p":
                        dst = o_yp[:, c, sg.t0:sg.t0 + sg.T]
                    else:
                        dst = o_ys[:, c, sg.s0 * TS:(sg.s0 + sg.S) * TS]
                    P.dma("sp", lambda e, c=c, dst=dst, tc=tc, sg=sg: e.dma_start(out=dst, in_=flat(tc, c // 4, c % 4)[:, sg.c0:sg.c0 + sg.n]),
                          key=f"o_S{c // 4}", reads=[sq_(c // 4, c % 4)])
        P.wait_all("sp")
        P.emit()
    return nc


_CACHE = {}


def _fm(vec, nch):
    return np.ascontiguousarray(vec.reshape(nch, 128).T)


def prep_inputs(inp, ncores=8):
    f32 = np.float32
    g = {k: np.asarray(v, dtype=f32) for k, v in inp.items()}
    pp = np.zeros((128, NPP), f32)
    for l in range(NL):
        def put(name, arr):
            o = l * PPL + PPO[name]
            pp[:, o:o + arr.shape[1]] = arr
        put("g_mix", _fm(g["g_mix"][l], 8))
        put("g_xattn", _fm(g["g_xattn"][l], 8))
        put("g_ffn", _fm(g["g_ffn"][l], 8))
        put("conv_a_w", g["conv_a_w"][l].reshape(31, 4, 128).transpose(2, 1, 0).reshape(128, 124))
        put("conv_a_b", _fm(g["conv_a_b"][l], 4))
        put("ln_a_g", _fm(g["ln_a_g"][l], 4))
        put("ln_a_b", _fm(g["ln_a_b"][l], 4))
        put("conv_b_w", g["conv_b_w"][l].reshape(3, 4, 128).transpose(2, 1, 0).reshape(128, 12))
        put("conv_b_b", _fm(g["conv_b_b"][l], 4))
        put("ln_c_g", _fm(g["ln_c_g"][l], 4))
        put("ln_c_b", _fm(g["ln_c_b"][l], 4))
        put("scale_d", _fm(g["scale_d"][l], 4))
        put("ffn_conv_w", g["ffn_conv_w"][l].reshape(3, 44, 128).transpose(2, 1, 0).reshape(128, 132))
        put("ffn_conv_b", _fm(g["ffn_conv_b"][l], 44))
    pp[:, NL * PPL:NL * PPL + 8] = _fm(g["g_final"], 8)
    wsT = np.ascontiguousarray(g["w_s"].transpose(0, 3, 1, 2))
    small = g["w_s"][:, :, :TS, :TS].transpose(0, 3, 1, 2)
    wsS = np.ascontiguousarray(np.broadcast_to(small[:, None, :, :, None, :], (NL, NS, TS, 4, NS, TS)).reshape(NL, 64, 4, 64))
    bsb = np.ascontiguousarray(np.broadcast_to(g["b_s"][:, None, :, :], (NL, 128, 4, 128)))
    bss = np.ascontiguousarray(np.broadcast_to(g["b_s"][:, None, :, None, :TS], (NL, 128, 4, NS, TS)).reshape(NL, 128, 4, 64))
    wdg = np.ascontiguousarray(g["w_d_grp"].transpose(0, 2, 1, 3))
    shared = dict(pp=pp, wsT=wsT, wsS=wsS, bsb=bsb, bss=bss, wdg=wdg)
    for k in ("w_in", "w_a_out", "w_b_out", "w_c_out", "w_d_out", "w_mix_out", "w_q", "w_mk", "w_mv", "w_xo", "w_up", "w_down"):
        shared[k] = np.ascontiguousarray(g[k])

    def fmaj(x):
        t = x.shape[0]
        return np.ascontiguousarray(x.reshape(t, 8, 128).transpose(2, 1, 0))

    in_maps = []
    for b in range(ncores):
        sl = slice(b * NS, (b + 1) * NS)
        m = dict(shared)
        m["xpT"] = fmaj(g["x_prompt"][b])
        m["xsT"] = fmaj(g["x_sample"][sl].reshape(NS * TS, D))
        m["memT"] = fmaj(g["mem_prompt"][b])

        def st_fm(a, nch):
            l_, s_, r_, c_ = a.shape
            return np.ascontiguousarray(a.reshape(l_, s_, r_, nch, 128).transpose(0, 4, 3, 1, 2))
        m["st_a"] = st_fm(g["state_conv_a"][:, sl], 4)
        m["st_b"] = st_fm(g["state_conv_b"][:, sl], 4)
        m["st_d"] = st_fm(g["state_pool_d"][:, sl], 4)
        sf = st_fm(g["state_ffn_conv"][:, sl], 44)
        m["st_f"] = np.ascontiguousarray(sf.reshape(NL, 128, 44, 2, NS // 2, 2).transpose(0, 1, 3, 2, 4, 5))
        ck = g["cache_mem_k"][:, sl].reshape(NL, NS, 256, 8, 128)
        m["ckT"] = np.ascontiguousarray(ck.transpose(0, 1, 4, 3, 2))
        cv = g["cache_mem_v"][:, sl].reshape(NL, NS, 2, 128, 1024)
        m["cvN"] = np.ascontiguousarray(cv.transpose(0, 1, 3, 2, 4))
        in_maps.append(m)
    return in_maps


def kernel(**inp):
    ncores = 8
    in_maps = prep_inputs(inp, ncores)
    if "nc" not in _CACHE:
        _CACHE["nc"] = build_program()
    nc = _CACHE["nc"]
    res = run_bass_kernel_spmd(nc, in_maps, core_ids=list(range(ncores)))
    return assemble(res.results)


def assemble(rs):
    f32 = np.float32

    def tokmaj(a):
        return a.transpose(2, 1, 0).reshape(a.shape[2], -1)

    def st_back(a):
        l_, p_, c_, r_ = a.shape
        return a.transpose(0, 3, 2, 1).reshape(l_, r_, c_ * 128)

    def sts_back(a):
        l_, p_, c_, s_, r_ = a.shape
        return a.transpose(0, 3, 4, 2, 1).reshape(l_, s_, r_, c_ * 128)

    y_p = np.stack([tokmaj(r["o_yp"]) for r in rs]).astype(f32)
    y_s = np.concatenate([tokmaj(r["o_ys"]).reshape(NS, TS, D) for r in rs]).astype(f32)
    a_p = np.stack([st_back(r["o_ap"]) for r in rs], axis=1)
    b_p = np.stack([st_back(r["o_bp"]) for r in rs], axis=1)
    d_p = np.stack([st_back(r["o_dp"]) for r in rs], axis=1)
    f_p = np.stack([st_back(r["o_fp"]) for r in rs], axis=1)
    v_p = np.stack([st_back(r["o_vp"]) for r in rs], axis=1)
    mk_p = np.stack([r["o_mk"].transpose(0, 3, 2, 1).reshape(NL, 256, 4, 256) for r in rs], axis=1)
    mv_p = np.stack([r["o_mv"].transpose(0, 2, 1, 3).reshape(NL, 256, 4, 256) for r in rs], axis=1)
    a_s = np.concatenate([sts_back(r["o_as"]) for r in rs], axis=1)
    b_s = np.concatenate([sts_back(r["o_bs"]) for r in rs], axis=1)
    d_s = np.concatenate([sts_back(r["o_ds"]) for r in rs], axis=1)
    f_s = np.concatenate([sts_back(r["o_fs"].transpose(0, 1, 3, 2, 4, 5).reshape(NL, 128, 44, NS, 2)) for r in rs], axis=1)
    v_s = np.concatenate([r["o_vs"].reshape(NL, 128, 4, NS, TS).transpose(0, 3, 4, 2, 1).reshape(NL, NS, TS, 512) for r in rs], axis=1)
    outs = (y_p, y_s, a_p, b_p, d_p, f_p, v_p, mk_p, mv_p, a_s, b_s, d_s, f_s, v_s)
    return tuple(np.ascontiguousarray(o, dtype=f32) for o in outs)
```

```python
import contextlib
import numpy as np
import concourse.bass as bass
import concourse.mybir as mybir
from concourse.bass_utils import run_bass_kernel_spmd

F32 = mybir.dt.float32
BF16 = mybir.dt.bfloat16
AF = mybir.ActivationFunctionType
ALU = mybir.AluOpType

ENGS = ("pe", "act", "dve", "pool", "sp")
SEM_ROT = 2500
SELF_DIST = 8


class Prog:
    def __init__(self, nc):
        self.nc = nc
        self.ops = {e: [] for e in ENGS}
        self.last_write = {}
        self.readers = {}
        self.seen = {e: {} for e in ENGS}
        self.dma_cnt = {}
        self.dma_keys = []
        self.last_c = {e: -1 for e in ENGS}
        self.tag = ""
        self.know = {e: {} for e in ENGS}
        self.opknow = {}

    def _need(self, eng, tok, waits, is_dma_issue=False, my_dma_key=None):
        if tok[0] == "e":
            _, p, i = tok
            if p == eng:
                if eng in ("act", "dve", "pool") and (is_dma_issue or i >= len(self.ops[eng]) - SELF_DIST):
                    if self.seen[eng].get(p, -1) >= i:
                        return
                    self.seen[eng][p] = i
                    waits.append(tok)
                    self.ops[p][i]["signal"] = True
                return
            if self.seen[eng].get(p, -1) >= i or self.know[eng].get(p, -1) >= i:
                return
            self.seen[eng][p] = i
            waits.append(tok)
            self.ops[p][i]["signal"] = True
            k = self.know[eng]
            for q, v in self.opknow.get((p, i), {}).items():
                if q != eng and k.get(q, -1) < v:
                    k[q] = v
            if k.get(p, -1) < i:
                k[p] = i
        else:
            _, key, cnt = tok
            if my_dma_key is not None and key == my_dma_key:
                return
            cnt = self.dma_cnt[key]
            if self.seen[eng].get(key, 0) >= cnt:
                return
            self.seen[eng][key] = cnt
            waits.append(("d", key, cnt))

    def _deps(self, eng, reads, writes, is_dma_issue=False, my_dma_key=None):
        waits = []
        for r in reads:
            t = self.last_write.get(r)
            if t is not None:
                self._need(eng, t, waits, is_dma_issue, my_dma_key)
        for r in writes:
            t = self.last_write.get(r)
            if t is not None:
                self._need(eng, t, waits, is_dma_issue, my_dma_key)
            for t in self.readers.get(r, {}).values():
                self._need(eng, t, waits, is_dma_issue, my_dma_key)
        return waits

    def _commit(self, tok, reads, writes, rkey):
        for r in reads:
            self.readers.setdefault(r, {})[rkey] = tok
        for r in writes:
            self.last_write[r] = tok
            self.readers[r] = {}

    def op(self, eng, fn, reads=(), writes=()):
        waits = self._deps(eng, reads, writes)
        idx = len(self.ops[eng])
        self.ops[eng].append(dict(kind="c", fn=fn, waits=waits, signal=False, tag=self.tag))
        self.last_c[eng] = idx
        self.opknow[(eng, idx)] = dict(self.know[eng])
        self._commit(("e", eng, idx), reads, writes, eng)

    def dma(self, eng, fn, key, reads=(), writes=()):
        if key not in self.dma_cnt:
            self.dma_cnt[key] = 0
            self.dma_keys.append(key)
        waits = self._deps(eng, reads, writes, is_dma_issue=True, my_dma_key=key)
        self.dma_cnt[key] += 16
        self.ops[eng].append(dict(kind="d", fn=fn, waits=waits, signal=False, key=key))
        self._commit(("d", key, self.dma_cnt[key]), reads, writes, key)

    def wait_all(self, eng):
        waits = []
        for k in self.dma_keys:
            if self.dma_cnt[k] > 0:
                self._need(eng, ("d", k, self.dma_cnt[k]), waits)
        self.ops[eng].append(dict(kind="w", fn=None, waits=waits, signal=False))

    def emit(self):
        nc = self.nc
        with contextlib.ExitStack() as st:
            sigval, esems = {}, {}
            for e in ENGS:
                c = 0
                for i, o in enumerate(self.ops[e]):
                    if o["signal"]:
                        c += 1
                        sigval[(e, i)] = c
                nsem = (c + SEM_ROT - 1) // SEM_ROT
                esems[e] = [st.enter_context(nc.semaphore(f"s_{e}{k}")) for k in range(nsem)]
            dsems = {k: st.enter_context(nc.semaphore(f"d_{k}")) for k in self.dma_keys}

            def resolve(tok):
                if tok[0] == "e":
                    v = sigval[(tok[1], tok[2])]
                    return esems[tok[1]][(v - 1) // SEM_ROT], (v - 1) % SEM_ROT + 1
                return dsems[tok[1]], tok[2]

            block = st.enter_context(nc.Block())

            def run(engname):
                def body(eng):
                    for i, o in enumerate(self.ops[engname]):
                        for t in o["waits"]:
                            s, v = resolve(t)
                            eng.wait_ge(s, v)
                        if o["kind"] == "w":
                            continue
                        inst = o["fn"](eng)
                        if o["kind"] == "d":
                            inst.then_inc(dsems[o["key"]], 16)
                        elif o["signal"]:
                            s, v = resolve(("e", engname, i))
                            inst.then_inc(s, 1)
                return body

            block.tensor(run("pe"))
            block.scalar(run("act"))
            block.vector(run("dve"))
            block.gpsimd(run("pool"))
            block.sync(run("sp"))


D = 1024
NL = 2
SEQ = 2048
NT = 512
NS = 16
TS = 4
DFF = 2816
EPS = 1e-6
BLK = 560
NSLOT = 5
SLOTW = 4096

_PP_FIELDS = [("g_mix", 8), ("g_xattn", 8), ("g_ffn", 8), ("conv_a_w", 124), ("conv_a_b", 4), ("ln_a_g", 4),
              ("ln_a_b", 4), ("conv_b_w", 12), ("conv_b_b", 4), ("ln_c_g", 4), ("ln_c_b", 4), ("scale_d", 4),
              ("ffn_conv_w", 132), ("ffn_conv_b", 44)]
PPO = {}
_o = 0
for _n, _w in _PP_FIELDS:
    PPO[_n] = _o
    _o += _w
PPL = _o
NPP = PPL * NL + 8


class Seg:
    def __init__(self, kind, S, T, c0, t0=0, s0=0, prevT=None, last=False):
        self.kind, self.S, self.T, self.c0 = kind, S, T, c0
        self.n = S * T
        self.t0, self.s0, self.prevT, self.last = t0, s0, prevT, last

    def ebase(self, H):
        return 0 if self.prevT is None else H + self.prevT


class TC:
    def __init__(self, idx, segs):
        self.idx = idx
        self.segs = segs
        self.N = sum(s.n for s in segs)

    def extlen(self, H):
        return sum(s.S * (H + s.T) for s in self.segs)


def build_program(tile_sel=None, nlayers=NL):
    nc = bass.Bass("TRN2", target_bir_lowering=False)

    def din(name, shape):
        return nc.dram_tensor(name, list(shape), F32, kind="ExternalInput").ap()

    def dout(name, shape):
        return nc.dram_tensor(name, list(shape), F32, kind="ExternalOutput").ap()

    xpT = din("xpT", [128, 8, SEQ])
    xsT = din("xsT", [128, 8, NS * TS])
    memT_d = din("memT", [128, 8, 256])
    st_a = din("st_a", [NL, 128, 4, NS, 30])
    st_b = din("st_b", [NL, 128, 4, NS, 2])
    st_d = din("st_d", [NL, 128, 4, NS, 15])
    st_f = din("st_f", [NL, 128, 2, 44, NS // 2, 2])
    ckT = din("ckT", [NL, NS, 128, 8, 256])
    cvN = din("cvN", [NL, NS, 128, 2, 1024])
    w_in = din("w_in", [NL, D, 8192])
    w_a_out = din("w_a_out", [NL, 512, D])
    w_b_out = din("w_b_out", [NL, 512, D])
    w_c_out = din("w_c_out", [NL, 512, D])
    w_d_out = din("w_d_out", [NL, 512, D])
    w_mix = din("w_mix_out", [NL, D, D])
    w_q = din("w_q", [NL, D, D])
    w_mk = din("w_mk", [NL, D, D])
    w_mv = din("w_mv", [NL, D, D])
    w_xo = din("w_xo", [NL, D, D])
    w_up = din("w_up", [NL, D, 2 * DFF])
    w_down = din("w_down", [NL, DFF, D])
    pp_d = din("pp", [128, NPP])
    wsT_d = din("wsT", [NL, 128, 4, 128])
    wsS_d = din("wsS", [NL, 64, 4, 64])
    bsb_d = din("bsb", [NL, 128, 4, 128])
    bss_d = din("bss", [NL, 128, 4, 64])
    wdg_d = din("wdg", [NL, 128, 4, 128])

    o_yp = dout("o_yp", [128, 8, SEQ])
    o_ys = dout("o_ys", [128, 8, NS * TS])
    o_ap = dout("o_ap", [NL, 128, 4, 30])
    o_bp = dout("o_bp", [NL, 128, 4, 2])
    o_dp = dout("o_dp", [NL, 128, 4, 15])
    o_fp = dout("o_fp", [NL, 128, 44, 2])
    o_vp = dout("o_vp", [NL, 128, 4, 128])
    o_mk = dout("o_mk", [NL, 128, 8, 256])
    o_mv = dout("o_mv", [NL, 128, 2, 1024])
    o_as = dout("o_as", [NL, 128, 4, NS, 30])
    o_bs = dout("o_bs", [NL, 128, 4, NS, 2])
    o_ds = dout("o_ds", [NL, 128, 4, NS, 15])
    o_fs = dout("o_fs", [NL, 128, 2, 44, NS // 2, 2])
    o_vs = dout("o_vs", [NL, 128, 4, NS * TS])

    st = contextlib.ExitStack()
    with st:
        def sb(name, shape, dt=F32):
            return st.enter_context(nc.sbuf_tensor(name, list(shape), dt))

        XT = sb("XT", [128, 8, NT])
        HB = sb("HB", [128, 8, NT], BF16)
        SQ = sb("SQ", [128, 8, NT], BF16)
        PPt = sb("PP", [128, NPP])
        IDENT = sb("IDENT", [128, 128])
        ONES = sb("ONES", [128, 3, 128], BF16)
        DUMMY = sb("DUMMY", [128, 2])
        KT = sb("KT", [128, NL, 8, 256], BF16)
        VT = sb("VT", [128, NL, 2, 1024], BF16)
        WDG = sb("WDG", [128, NL, 4, 128], BF16)
        WST = sb("WST", [128, NL, 4, 128], BF16)
        WSS = sb("WSS", [64, NL, 4, 64], BF16)
        BSB = sb("BSB", [128, NL, 4, 128])
        BSS = sb("BSS", [128, NL, 4, 64])
        HA = sb("HA", [128, NL, 4, 30])
        HBh = sb("HBh", [128, NL, 4, 2])
        HD = sb("HD", [128, NL, 4, 15])
        HF = sb("HF", [128, NL, 44, 2])
        HFS = sb("HFS", [128, 2, 44, NS // 2, 2])
        INVC = sb("INVC", [128, 4, 15])
        STAT = [sb(f"STAT{i}", [128, NT]) for i in range(2)]
        GS = [sb(f"GS{i}", [128, NT]) for i in range(4)]
        MB = sb("MB", [128, 8, NT], BF16)
        VNT = sb("VNT", [128, 4, 4, 128], BF16)
        YIN = [sb(f"YIN{i}", [128, 4, NT], BF16) for i in range(4)]
        WS = [sb(f"WS{i}", [128, SLOTW], BF16) for i in range(NSLOT)]
        SB_ = [sb(f"S{i}", [128, 4 * BLK]) for i in range(5)]
        SBb = [t.bitcast(BF16) for t in SB_]
        STS = sb("STS", [128, 4 * 8 * 30])
        STG = sb("STG", [128, 8, 256])

        PS = [st.enter_context(nc.psum_tensor(f"PS{i}", [128, 512], F32)) for i in range(8)]

        P = Prog(nc)
        cnt = dict(ps=0, slot=0, gs=0, scr=0)

        def R(name, n):
            return [f"{name}{i}" for i in range(n)]

        def next_ps():
            b = cnt["ps"] % 8
            cnt["ps"] += 1
            return b

        def next_gs():
            b = cnt["gs"] % 4
            cnt["gs"] += 1
            return b

        def mm(ps_ap, pairs, reads, psreg):
            pairs = list(pairs)

            def fn(e):
                n = len(pairs)
                ins = None
                for i, (l, r) in enumerate(pairs):
                    ins = e.matmul(ps_ap, l, r, start=(i == 0), stop=(i == n - 1))
                return ins
            P.op("pe", fn, reads=reads, writes=[psreg])

        def load_w(src_ap, kc, ncols):
            s = cnt["slot"] % NSLOT
            cnt["slot"] += 1
            view = WS[s][:, 0:kc * ncols].rearrange("p (k n) -> p k n", k=kc)
            P.dma("pool", lambda e: e.dma_start(out=view, in_=src_ap), key=f"w{s}", writes=[f"ws{s}"])
            return view, f"ws{s}"

        def wsrc(w_l, c0, ncols):
            return w_l.rearrange("(k p) n -> p k n", p=128)[:, :, c0:c0 + ncols]

        def ppc(l, name, j, w=1):
            o = l * PPL + PPO[name] + j
            return PPt[:, o:o + w]

        def sq_(blk, c):
            return f"S{blk}q{c}"

        def sblk(blk):
            return [sq_(blk, c) for c in range(4)]

        def flat(tc, blk, c):
            return SB_[blk][:, c * BLK: c * BLK + tc.N]

        def quarters(blk, a, b, nch=4):
            return SB_[blk][:, 0:4 * BLK].rearrange("p (c x) -> p c x", c=4)[:, 0:nch, a:b]

        def fseg(sg, ap):
            a = ap[:, sg.c0:sg.c0 + sg.n]
            return a if sg.S == 1 else a.rearrange("p (s t) -> p s t", s=sg.S)

        def ext(sg, blk, c, H, a, b):
            L = H + sg.T
            o = c * BLK + sg.ebase(H)
            base = SB_[blk][:, o:o + sg.S * L]
            if sg.S == 1:
                return base[:, a:b]
            return base.rearrange("p (s j) -> p s j", s=sg.S)[:, :, a:b]

        def bf8(blk, j, n):
            o = (j // 2) * (2 * BLK) + (j % 2) * 512
            return SBb[blk][:, o:o + n]

        def bf8r(blk, j):
            return sq_(blk, j // 2)

        P.op("pool", lambda e: e.memset(IDENT[:], 0.0), writes=["ident"])
        P.op("pool", lambda e: e.affine_select(out=IDENT[:], in_=IDENT[:], compare_op=ALU.not_equal, fill=1.0,
                                               base=0, pattern=[[-1, 128]], channel_multiplier=1),
             reads=["ident"], writes=["ident"])
        P.op("dve", lambda e: e.memset(ONES[:, 0, :], 1.0 / 1024), writes=["ones"])
        P.op("dve", lambda e: e.memset(ONES[:, 1, :], 1.0 / 512), writes=["ones"])
        P.op("dve", lambda e: e.memset(ONES[:, 2, :], 1.0), writes=["ones"])
        for t_, nm in ((HA, "ha"), (HBh, "hb_"), (HD, "hd"), (HF, "hf")):
            P.op("dve", lambda e, t_=t_: e.memset(t_[:], 0.0), writes=[nm + "0", nm + "1"])
        for t in range(15):
            P.op("dve", lambda e, t=t: e.memset(INVC[:, :, t:t + 1], 1.0 / (t + 1)), writes=["invc"])
        for c in range(3):
            w = 2 << c
            P.op("dve", lambda e, c=c, w=w: e.memset(INVC[:, c, w - 1:15], 1.0 / w), writes=["invc"])
        P.dma("sp", lambda e: e.dma_start(out=PPt[:], in_=pp_d), key="cst_pp", writes=["pp"])
        P.dma("sp", lambda e: e.dma_start(out=BSB[:], in_=bsb_d.rearrange("l p g t -> p l g t")), key="cst_bsb", writes=["bsb"])
        P.dma("sp", lambda e: e.dma_start(out=BSS[:], in_=bss_d.rearrange("l p g t -> p l g t")), key="cst_bss", writes=["bss"])
        P.dma("pool", lambda e: e.dma_start(out=WDG[:], in_=wdg_d.rearrange("l p g d -> p l g d")), key="cst2", writes=["wdg"])
        P.dma("sp", lambda e: e.dma_start(out=SB_[0][:, 0:NL * 512].rearrange("p (l g t) -> p l g t", l=NL, g=4),
                                          in_=wsT_d.rearrange("l p g t -> p l g t")), key="in_S0", writes=sblk(0))
        P.op("pool", lambda e: e.affine_select(out=SB_[0][:, 0:NL * 512].rearrange("p (l g t) -> p l g t", l=NL, g=4),
                                               in_=SB_[0][:, 0:NL * 512].rearrange("p (l g t) -> p l g t", l=NL, g=4),
                                               compare_op=ALU.is_ge, fill=0.0, base=0,
                                               pattern=[[0, NL], [0, 4], [1, 128]], channel_multiplier=-1),
             reads=sblk(0), writes=sblk(0))
        P.op("dve", lambda e: e.tensor_copy(WST[:], SB_[0][:, 0:NL * 512].rearrange("p (l g t) -> p l g t", l=NL, g=4)),
             reads=sblk(0), writes=["wst"])
        S1v = SB_[1][0:64, 0:NL * 256].rearrange("p (l g a t) -> p l g a t", l=NL, g=4, a=NS)
        P.dma("sp", lambda e: e.dma_start(out=SB_[1][0:64, 0:NL * 256].rearrange("p (l g f) -> p l g f", l=NL, g=4),
                                          in_=wsS_d.rearrange("l p g f -> p l g f")), key="in_S1", writes=sblk(1))
        P.op("pool", lambda e: e.affine_select(out=S1v, in_=S1v, compare_op=ALU.is_ge, fill=0.0, base=0,
                                               pattern=[[0, NL], [0, 4], [4, NS], [1, TS]], channel_multiplier=-1),
             reads=sblk(1), writes=sblk(1))
        P.op("pool", lambda e: e.affine_select(out=S1v, in_=S1v, compare_op=ALU.is_ge, fill=0.0, base=0,
                                               pattern=[[0, NL], [0, 4], [-4, NS], [0, TS]], channel_multiplier=1),
             reads=sblk(1), writes=sblk(1))
        P.op("dve", lambda e: e.tensor_copy(WSS[:], SB_[1][0:64, 0:NL * 256].rearrange("p (l g f) -> p l g f", l=NL, g=4)),
             reads=sblk(1), writes=["wss"])

        P.tag = "prologue"
        P.dma("sp", lambda e: e.dma_start(out=STG[:], in_=memT_d), key="in_STG", writes=["stg"])
        for j in range(8):
            P.op("act", lambda e, j=j: e.copy(bf8(2, j, 256), STG[:, j, :]), reads=["stg"], writes=[bf8r(2, j)])
        for l in range(NL):
            for half in range(2):
                wv, wr = load_w(wsrc(w_mk[l], half * 512, 512), 8, 512)
                for jj in range(4):
                    j = half * 4 + jj
                    pb = next_ps()
                    mm(PS[pb][:, 0:256], [(wv[:, kc, jj * 128:(jj + 1) * 128], bf8(2, kc, 256)) for kc in range(8)],
                       reads=[wr] + sblk(2), psreg=f"ps{pb}")
                    P.op("act", lambda e, pb=pb, j=j: e.copy(STG[:, j, :], PS[pb][:, 0:256]),
                         reads=[f"ps{pb}"], writes=["stg"])
                    P.op("dve", lambda e, j=j, l=l: e.tensor_copy(KT[:, l, j, :], STG[:, j, :]),
                         reads=["stg"], writes=["kt"])
            P.dma("sp", lambda e, l=l: e.dma_start(out=o_mk[l], in_=STG[:]), key="o_STG", reads=["stg"])
            for half in range(2):
                wv, wr = load_w(wsrc(w_mv[l], half * 512, 512), 8, 512)
                for mc in range(2):
                    pb = next_ps()
                    mm(PS[pb][:, 0:512], [(bf8(2, kc, 256)[:, mc * 128:(mc + 1) * 128], wv[:, kc, :]) for kc in range(8)],
                       reads=[wr] + sblk(2), psreg=f"ps{pb}")
                    P.op("act", lambda e, pb=pb, mc=mc, half=half: e.copy(SB_[3 + mc][:, half * 512:(half + 1) * 512], PS[pb][:, 0:512]),
                         reads=[f"ps{pb}"], writes=sblk(3 + mc))
                    P.op("dve", lambda e, mc=mc, half=half, l=l: e.tensor_copy(VT[:, l, mc, half * 512:(half + 1) * 512], SB_[3 + mc][:, half * 512:(half + 1) * 512]),
                         reads=sblk(3 + mc), writes=["vt"])
            for mc in range(2):
                P.dma("sp", lambda e, l=l, mc=mc: e.dma_start(out=o_mv[l, :, mc, :], in_=SB_[3 + mc][:, 0:1024]),
                      key=f"o_S{3 + mc}", reads=sblk(3 + mc))

        def preswitch(func):
            P.op("act", lambda e: e.activation(DUMMY[:, 1:2], ONES[:, 2, 0:1], func), reads=["ones"], writes=["dummy2"])

        def preswitch_ln():
            P.op("act", lambda e: e.activation(DUMMY[:, 0:1], ONES[:, 2, 0:1], AF.Ln), reads=["ones"], writes=["dummy"])

        def rmsnorm(tc, gname, l, out_bf=True, out_fn=None, next_func=None):
            N = tc.N
            preswitch_ln()
            P.op("act", lambda e: e.activation(SQ[:, 0:4, 0:N], XT[:, 0:4, 0:N], AF.Square), reads=R("x", 8)[0:4], writes=["sqa"])
            P.op("dve", lambda e: e.tensor_tensor(out=SQ[:, 4:8, 0:N], in0=XT[:, 4:8, 0:N], in1=XT[:, 4:8, 0:N], op=ALU.mult),
                 reads=R("x", 8)[4:8], writes=["sqb"])
            pb = next_ps()
            mm(PS[pb][:, 0:N], [(ONES[:, 0, :], SQ[:, c, 0:N]) for c in range(8)], reads=["sqa", "sqb", "ones"], psreg=f"ps{pb}")
            P.op("act", lambda e: e.activation(STAT[0][:, 0:N], PS[pb][:, 0:N], AF.Ln, bias=EPS), reads=[f"ps{pb}"], writes=["stat0"])
            P.op("act", lambda e: e.activation(STAT[0][:, 0:N], STAT[0][:, 0:N], AF.Exp, scale=-0.5), reads=["stat0"], writes=["stat0"])
            if next_func is not None:
                preswitch(next_func)
            for c in range(8):
                if l is None:
                    g = PPt[:, NL * PPL + c: NL * PPL + c + 1]
                else:
                    g = ppc(l, gname, c)
                if out_fn is None:
                    P.op("dve", lambda e, c=c, g=g: e.scalar_tensor_tensor(out=HB[:, c, 0:N], in0=XT[:, c, 0:N], scalar=g,
                                                                           in1=STAT[0][:, 0:N], op0=ALU.mult, op1=ALU.mult),
                         reads=[f"x{c}", "stat0", "pp"], writes=[f"hb{c}"])
                else:
                    oap, oreg = out_fn(c)
                    P.op("dve", lambda e, c=c, g=g, oap=oap: e.scalar_tensor_tensor(out=oap, in0=XT[:, c, 0:N], scalar=g,
                                                                                    in1=STAT[0][:, 0:N], op0=ALU.mult, op1=ALU.mult),
                         reads=[f"x{c}", "stat0", "pp"], writes=[oreg])

        def ln_stats(tc, blk, next_func=None):
            N = tc.N
            preswitch_ln()
            P.op("act", lambda e: e.copy(SQ[:, 0:4, 0:N], quarters(blk, 0, N)), reads=sblk(blk), writes=["sqa"])
            P.op("act", lambda e: e.activation(SQ[:, 4:8, 0:N], quarters(blk, 0, N), AF.Square), reads=sblk(blk), writes=["sqb"])
            pm = next_ps()
            mm(PS[pm][:, 0:N], [(ONES[:, 1, :], SQ[:, c, 0:N]) for c in range(4)], reads=["sqa", "ones"], psreg=f"ps{pm}")
            pq = next_ps()
            mm(PS[pq][:, 0:N], [(ONES[:, 1, :], SQ[:, 4 + c, 0:N]) for c in range(4)], reads=["sqb", "ones"], psreg=f"ps{pq}")
            P.op("act", lambda e: e.activation(STAT[1][:, 0:N], PS[pm][:, 0:N], AF.Square), reads=[f"ps{pm}"], writes=["stat1"])
            P.op("dve", lambda e: e.tensor_tensor(out=STAT[1][:, 0:N], in0=PS[pq][:, 0:N], in1=STAT[1][:, 0:N], op=ALU.subtract),
                 reads=[f"ps{pq}", "stat1"], writes=["stat1"])
            P.op("act", lambda e: e.activation(STAT[1][:, 0:N], STAT[1][:, 0:N], AF.Ln, bias=EPS), reads=["stat1"], writes=["stat1"])
            P.op("act", lambda e: e.activation(STAT[1][:, 0:N], STAT[1][:, 0:N], AF.Exp, scale=-0.5), reads=["stat1"], writes=["stat1"])
            if next_func is not None:
                preswitch(next_func)
            return pm

        def hist_in(tc, l, blk, H, hist_t, hist_nm, st_d_ap, nch=4):
            for sg in tc.segs:
                if sg.kind == "p":
                    P.op("act", lambda e: e.copy(quarters(blk, 0, H, nch), hist_t[:, l, :, :]),
                         reads=[f"{hist_nm}{l}"], writes=sblk(blk)[0:nch])
                else:
                    SH = sg.S * H
                    stg = STS[:, 0:nch * SH].rearrange("p (c x) -> p c x", c=nch)
                    P.dma("sp", lambda e, sg=sg, stg=stg: e.dma_start(out=stg, in_=st_d_ap[l, :, :, sg.s0:sg.s0 + sg.S, :].rearrange("p c s h -> p c (s h)")),
                          key="in_STS", writes=["sts"])
                    for c in range(nch):
                        P.op("act", lambda e, c=c, sg=sg, stg=stg: e.copy(ext(sg, blk, c, H, 0, H), stg[:, c, :].rearrange("p (s h) -> p s h", s=sg.S)),
                             reads=["sts"], writes=[sq_(blk, c)])

        def hist_out(tc, l, blk, H, hist_t, hist_nm, o_p, o_s, nch=4):
            for sg in tc.segs:
                T = sg.T
                if sg.kind == "p":
                    P.op("act", lambda e, T=T: e.copy(hist_t[:, l, :, :], quarters(blk, T, T + H, nch)),
                         reads=sblk(blk)[0:nch], writes=[f"{hist_nm}{l}"])
                    if sg.last:
                        P.dma("sp", lambda e: e.dma_start(out=o_p[l], in_=hist_t[:, l, :, :]), key=f"o_{hist_nm}", reads=[f"{hist_nm}{l}"])
                else:
                    SH = sg.S * H
                    stg = STS[:, 0:nch * SH].rearrange("p (c x) -> p c x", c=nch)
                    for c in range(nch):
                        P.op("act", lambda e, c=c, sg=sg, T=T, stg=stg: e.copy(stg[:, c, :].rearrange("p (s h) -> p s h", s=sg.S), ext(sg, blk, c, H, T, T + H)),
                             reads=[sq_(blk, c)], writes=["sts"])
                    P.dma("sp", lambda e, sg=sg, stg=stg: e.dma_start(out=o_s[l, :, :, sg.s0:sg.s0 + sg.S, :].rearrange("p c s h -> p c (s h)"), in_=stg),
                          key="o_STS", reads=["sts"])

        def layer(tc, l):
            N = tc.N
            segs = tc.segs

            def zmm(wv, wr, jj):
                pb = next_ps()
                mm(PS[pb][:, 0:N], [(wv[:, kc, jj * 128:(jj + 1) * 128], HB[:, kc, 0:N]) for kc in range(8)],
                   reads=[wr] + R("hb", 8), psreg=f"ps{pb}")
                return pb

            def win_unit(u):
                return load_w(wsrc(w_in[l], u * 512, 512), 8, 512)

            def conv_taps(src_blk, dst_blk, H, wname, bname, ntap):
                for k in range(ntap):
                    for c in range(4):
                        wk = ppc(l, wname, c * ntap + k)
                        for sg in segs:
                            T = sg.T
                            if k == 0:
                                P.op("dve", lambda e, c=c, wk=wk, sg=sg, T=T: e.tensor_scalar(out=fseg(sg, flat(tc, dst_blk, c)), in0=ext(sg, src_blk, c, H, 0, T),
                                                                                                 scalar1=wk, scalar2=ppc(l, bname, c), op0=ALU.mult, op1=ALU.add),
                                     reads=[sq_(src_blk, c), "pp"], writes=[sq_(dst_blk, c)])
                            else:
                                P.op("dve", lambda e, c=c, wk=wk, k=k, sg=sg, T=T: e.scalar_tensor_tensor(out=fseg(sg, flat(tc, dst_blk, c)), in0=ext(sg, src_blk, c, H, k, k + T),
                                                                                                             scalar=wk, in1=fseg(sg, flat(tc, dst_blk, c)), op0=ALU.mult, op1=ALU.add),
                                     reads=[sq_(src_blk, c), sq_(dst_blk, c), "pp"], writes=[sq_(dst_blk, c)])

            def ln_apply(blk, pm):
                P.op("dve", lambda e, pm=pm: e.tensor_tensor(out=quarters(blk, 0, N), in0=quarters(blk, 0, N),
                                                            in1=PS[pm][:, 0:N].unsqueeze(1).broadcast_to([128, 4, N]), op=ALU.subtract),
                     reads=sblk(blk) + [f"ps{pm}"], writes=sblk(blk))
                P.op("dve", lambda e: e.tensor_tensor(out=quarters(blk, 0, N), in0=quarters(blk, 0, N),
                                                      in1=STAT[1][:, 0:N].unsqueeze(1).broadcast_to([128, 4, N]), op=ALU.mult),
                     reads=sblk(blk) + ["stat1"], writes=sblk(blk))

            P.tag = "norm1"
            rmsnorm(tc, "g_mix", l, next_func=AF.Sigmoid)

            P.tag = "A"
            wv, wr = win_unit(1)
            for c in range(4):
                pb = zmm(wv, wr, c)
                P.op("act", lambda e, pb=pb, c=c: e.activation(flat(tc, 0, c), PS[pb][:, 0:N], AF.Sigmoid),
                     reads=[f"ps{pb}"], writes=[sq_(0, c)])
            hist_in(tc, l, 1, 30, HA, "ha", st_a)
            wv, wr = win_unit(0)
            for c in range(4):
                pb = zmm(wv, wr, c)
                for sg in segs:
                    P.op("dve", lambda e, pb=pb, c=c, sg=sg: e.tensor_tensor(out=ext(sg, 1, c, 30, 30, 30 + sg.T), in0=fseg(sg, PS[pb][:, 0:N]),
                                                                             in1=fseg(sg, flat(tc, 0, c)), op=ALU.mult),
                         reads=[f"ps{pb}", sq_(0, c)], writes=[sq_(1, c)])
            hist_out(tc, l, 1, 30, HA, "ha", o_ap, o_as)
            ELA = tc.extlen(30)
            for cb_ in (0, 2):
                prep = {}
                for c in (cb_, cb_ + 1):
                    ub = SBb[0][:, c * 2 * BLK: c * 2 * BLK + ELA]
                    P.op("act", lambda e, c=c, ub=ub: e.copy(ub, SB_[1][:, c * BLK: c * BLK + ELA]),
                         reads=[sq_(1, c)], writes=[sq_(0, c)])
                    sl_ = cnt["slot"] % NSLOT
                    cnt["slot"] += 1
                    dw = WS[sl_][:, 0:31 * 128].rearrange("p (k j) -> p k j", k=31)
                    wofs = l * PPL + PPO["conv_a_w"] + c * 31
                    P.op("pool", lambda e, dw=dw, wofs=wofs: e.tensor_tensor(out=dw, in0=IDENT[:, :].unsqueeze(1).broadcast_to([128, 31, 128]),
                                                                             in1=PPt[:, wofs:wofs + 31].unsqueeze(2).broadcast_to([128, 31, 128]),
                                                                             op=ALU.mult),
                         reads=["ident", "pp"], writes=[f"ws{sl_}"])
                    prep[c] = (ub, dw, sl_)
                pbs = {}
                for c in (cb_, cb_ + 1):
                    ub, dw, sl_ = prep[c]
                    pb = next_ps()
                    pbs[c] = pb
                    for sg in segs:
                        LA = 30 + sg.T
                        ubs = ub[:, sg.ebase(30):sg.ebase(30) + sg.S * LA]
                        if sg.S == 1:
                            ubv = lambda k, ubs=ubs, sg=sg: ubs[:, k:k + sg.T]
                        else:
                            ubv = lambda k, ubs=ubs, sg=sg: ubs.rearrange("p (s j) -> p s j", s=sg.S)[:, :, k:k + sg.T]
                        mm(fseg(sg, PS[pb][:, 0:N]), [(dw[:, k, :], ubv(k)) for k in range(31)], reads=[f"ws{sl_}", sq_(0, c)], psreg=f"ps{pb}")
                for c in (cb_, cb_ + 1):
                    pb = pbs[c]
                    P.op("act", lambda e, pb=pb, c=c: e.activation(flat(tc, 2, c), PS[pb][:, 0:N], AF.Identity, bias=ppc(l, "conv_a_b", c)),
                         reads=[f"ps{pb}", "pp"], writes=[sq_(2, c)])
            pm = ln_stats(tc, 2, next_func=AF.Silu)
            ln_apply(2, pm)
            for c in range(4):
                P.op("act", lambda e, c=c: e.activation(YIN[0][:, c, 0:N], flat(tc, 2, c), AF.Silu,
                                                        bias=ppc(l, "ln_a_b", c), scale=ppc(l, "ln_a_g", c)),
                     reads=[sq_(2, c), "pp"], writes=[f"yin0_{c}"])

            P.tag = "B"
            wv, wr = win_unit(3)
            for c in range(4):
                pb = zmm(wv, wr, c)
                P.op("act", lambda e, pb=pb, c=c: e.copy(flat(tc, 3, c), PS[pb][:, 0:N]), reads=[f"ps{pb}"], writes=[sq_(3, c)])
            hist_in(tc, l, 4, 2, HBh, "hb_", st_b)
            wv, wr = win_unit(4)
            for c in range(4):
                pb = zmm(wv, wr, c)
                for sg in segs:
                    P.op("dve", lambda e, pb=pb, c=c, sg=sg: e.tensor_tensor(out=ext(sg, 4, c, 2, 2, 2 + sg.T), in0=fseg(sg, PS[pb][:, 0:N]),
                                                                             in1=fseg(sg, flat(tc, 3, c)), op=ALU.mult),
                         reads=[f"ps{pb}", sq_(3, c)], writes=[sq_(4, c)])
            hist_out(tc, l, 4, 2, HBh, "hb_", o_bp, o_bs)
            conv_taps(4, 0, 2, "conv_b_w", "conv_b_b", 3)
            wv, wr = win_unit(2)
            for c in range(4):
                pb = zmm(wv, wr, c)
                P.op("dve", lambda e, pb=pb, c=c: e.tensor_tensor(out=YIN[1][:, c, 0:N], in0=PS[pb][:, 0:N], in1=flat(tc, 0, c), op=ALU.mult),
                     reads=[f"ps{pb}", sq_(0, c)], writes=[f"yin1_{c}"])

            P.tag = "C"
            preswitch(AF.Gelu_apprx_tanh)
            wv, wr = win_unit(5)
            for c in range(4):
                pb = zmm(wv, wr, c)
                P.op("act", lambda e, pb=pb, c=c: e.activation(flat(tc, 1, c), PS[pb][:, 0:N], AF.Gelu_apprx_tanh),
                     reads=[f"ps{pb}"], writes=[sq_(1, c)])
            wv, wr = win_unit(6)
            for c in range(4):
                pb = zmm(wv, wr, c)
                P.op("act", lambda e, pb=pb, c=c: e.activation(flat(tc, 2, c), PS[pb][:, 0:N], AF.Gelu_apprx_tanh),
                     reads=[f"ps{pb}"], writes=[sq_(2, c)])
            pm = ln_stats(tc, 2, next_func=AF.Sigmoid)
            ln_apply(2, pm)
            for c in range(4):
                P.op("act", lambda e, c=c: e.activation(flat(tc, 2, c), flat(tc, 2, c), AF.Identity,
                                                        bias=ppc(l, "ln_c_b", c), scale=ppc(l, "ln_c_g", c)),
                     reads=[sq_(2, c), "pp"], writes=[sq_(2, c)])
            for sg in segs:
                if sg.kind == "p":
                    if sg.last:
                        for c in range(4):
                            P.dma("sp", lambda e, c=c, sg=sg: e.dma_start(out=o_vp[l, :, c, :], in_=flat(tc, 2, c)[:, sg.c0 + sg.T - 128:sg.c0 + sg.T]),
                                  key="o_S2", reads=[sq_(2, c)])
                else:
                    for c in range(4):
                        P.dma("sp", lambda e, c=c, sg=sg: e.dma_start(out=o_vs[l, :, c, sg.s0 * TS:(sg.s0 + sg.S) * TS],
                                                                       in_=flat(tc, 2, c)[:, sg.c0:sg.c0 + sg.n]),
                              key="o_S2", reads=[sq_(2, c)])
            pblocks = []
            for sg in segs:
                if sg.kind == "p":
                    for i in range(sg.T // 128):
                        pblocks.append((sg.c0 + i * 128, 128, "p"))
                else:
                    pblocks.append((sg.c0, sg.n, "s"))
            for pi, (pc0, PW, kd) in enumerate(pblocks):
                pb = next_ps()
                for c in range(4):
                    P.op("pe", lambda e, pb=pb, pc0=pc0, PW=PW, c=c: e.transpose(PS[pb][0:PW, c * 128:(c + 1) * 128],
                                                                                 flat(tc, 2, c)[:, pc0:pc0 + PW], IDENT[:]),
                         reads=[sq_(2, c), "ident"], writes=[f"ps{pb}"])
                P.op("act", lambda e, pb=pb, pi=pi, PW=PW: e.copy(VNT[0:PW, pi, :, :], PS[pb][0:PW, 0:512].rearrange("p (c d) -> p c d", c=4)),
                     reads=[f"ps{pb}"], writes=["vnt"])
            for c in range(4):
                pb = next_ps()
                g = next_gs()
                for pi, (pc0, PW, kd) in enumerate(pblocks):
                    rhs = WST[:, l, c, :] if kd == "p" else WSS[0:PW, l, c, 0:PW]
                    mm(PS[pb][:, pc0:pc0 + PW], [(VNT[0:PW, pi, c, :], rhs)], reads=["vnt", "wst", "wss"], psreg=f"ps{pb}")
                for pi, (pc0, PW, kd) in enumerate(pblocks):
                    bias = BSB[:, l, c, :] if kd == "p" else BSS[:, l, c, 0:PW]
                    P.op("dve", lambda e, pb=pb, pc0=pc0, PW=PW, g=g, bias=bias: e.tensor_tensor(out=GS[g][:, pc0:pc0 + PW], in0=PS[pb][:, pc0:pc0 + PW],
                                                                                               in1=bias, op=ALU.add),
                         reads=[f"ps{pb}", "bsb", "bss"], writes=[f"gs{g}"])
                P.op("dve", lambda e, c=c, g=g: e.tensor_tensor(out=YIN[2][:, c, 0:N], in0=GS[g][:, 0:N], in1=flat(tc, 1, c), op=ALU.mult),
                     reads=[f"gs{g}", sq_(1, c)], writes=[f"yin2_{c}"])

            P.tag = "D"
            hist_in(tc, l, 3, 15, HD, "hd", st_d)
            wv, wr = win_unit(7)
            for c in range(4):
                pb = zmm(wv, wr, c)
                for sg in segs:
                    P.op("act", lambda e, pb=pb, c=c, sg=sg: e.copy(ext(sg, 3, c, 15, 15, 15 + sg.T), fseg(sg, PS[pb][:, 0:N])),
                         reads=[f"ps{pb}"], writes=[sq_(3, c)])
            hist_out(tc, l, 3, 15, HD, "hd", o_dp, o_ds)
            for c in range(4):
                w = 2 << c
                for sg in segs:
                    L = 15 + sg.T
                    src_blk = 3
                    for k in range(c + 1):
                        d = 1 << k
                        lo = (2 << k) - 1
                        dst_blk = 4 if (k % 2 == 0) else 0
                        P.op("dve", lambda e, c=c, sg=sg, L=L, src_blk=src_blk, dst_blk=dst_blk, d=d, lo=lo:
                             e.tensor_tensor(out=ext(sg, dst_blk, c, 15, lo, L), in0=ext(sg, src_blk, c, 15, lo, L),
                                             in1=ext(sg, src_blk, c, 15, lo - d, L - d), op=ALU.add),
                             reads=[sq_(src_blk, c)], writes=[sq_(dst_blk, c)])
                        src_blk = dst_blk
                    P.op("dve", lambda e, c=c, sg=sg, L=L, src_blk=src_blk, w=w: e.scalar_tensor_tensor(out=fseg(sg, SQ[:, c, 0:N]), in0=ext(sg, src_blk, c, 15, 15, L),
                                                                                                     scalar=1.0 / w, in1=ext(sg, 3, c, 15, 15, L),
                                                                                                     op0=ALU.mult, op1=ALU.subtract),
                         reads=[sq_(src_blk, c), sq_(3, c)], writes=["sqa"])
                    if sg.kind == "p" and sg.t0 == 0:
                        g = next_gs()
                        P.op("dve", lambda e, c=c, sg=sg, src_blk=src_blk, g=g: e.tensor_tensor(out=GS[g][:, 0:15], in0=ext(sg, src_blk, c, 15, 15, 30),
                                                                                             in1=INVC[:, c, :], op=ALU.mult),
                             reads=[sq_(src_blk, c), "invc"], writes=[f"gs{g}"])
                        P.op("dve", lambda e, c=c, sg=sg, g=g: e.tensor_tensor(out=SQ[:, c, sg.c0:sg.c0 + 15], in0=GS[g][:, 0:15], in1=ext(sg, 3, c, 15, 15, 30), op=ALU.subtract),
                             reads=[f"gs{g}", sq_(3, c)], writes=["sqa"])
            for c in range(4):
                pb = next_ps()
                mm(PS[pb][:, 0:N], [(WDG[:, l, c, :], SQ[:, c, 0:N])], reads=["sqa", "wdg"], psreg=f"ps{pb}")
                P.op("act", lambda e, pb=pb, c=c: e.activation(YIN[3][:, c, 0:N], PS[pb][:, 0:N], AF.Identity, scale=ppc(l, "scale_d", c)),
                     reads=[f"ps{pb}", "pp"], writes=[f"yin3_{c}"])

            P.tag = "merge"
            for i, wo in enumerate((w_a_out, w_b_out, w_c_out, w_d_out)):
                wv, wr = load_w(wsrc(wo[l], 0, 1024), 4, 1024)
                for half in range(2):
                    gv, gr = win_unit(8 + 2 * i + half)
                    for oo in range(4):
                        o = half * 4 + oo
                        acc = flat(tc, 3 + o // 4, o % 4)
                        accr = sq_(3 + o // 4, o % 4)
                        py = next_ps()
                        mm(PS[py][:, 0:N], [(wv[:, kc, o * 128:(o + 1) * 128], YIN[i][:, kc, 0:N]) for kc in range(4)],
                           reads=[wr] + [f"yin{i}_{kc}" for kc in range(4)], psreg=f"ps{py}")
                        pg = zmm(gv, gr, oo)
                        g = next_gs()
                        P.op("act", lambda e, pg=pg, g=g: e.activation(GS[g][:, 0:N], PS[pg][:, 0:N], AF.Sigmoid),
                             reads=[f"ps{pg}"], writes=[f"gs{g}"])
                        if i == 0:
                            P.op("dve", lambda e, py=py, g=g, acc=acc: e.tensor_tensor(out=acc, in0=PS[py][:, 0:N], in1=GS[g][:, 0:N], op=ALU.mult),
                                 reads=[f"ps{py}", f"gs{g}"], writes=[accr])
                        else:
                            P.op("dve", lambda e, py=py, g=g: e.tensor_tensor(out=GS[g][:, 0:N], in0=PS[py][:, 0:N], in1=GS[g][:, 0:N], op=ALU.mult),
                                 reads=[f"ps{py}", f"gs{g}"], writes=[f"gs{g}"])
                            if i < 3:
                                P.op("dve", lambda e, g=g, acc=acc: e.tensor_tensor(out=acc, in0=acc, in1=GS[g][:, 0:N], op=ALU.add),
                                     reads=[accr, f"gs{g}"], writes=[accr])
                            else:
                                P.op("dve", lambda e, g=g, acc=acc, o=o: e.tensor_tensor(out=MB[:, o, 0:N], in0=acc, in1=GS[g][:, 0:N], op=ALU.add),
                                     reads=[accr, f"gs{g}"], writes=[f"mb{o}"])

            def proj_residual(w_l, src_fn, src_regs, nk):
                ncol = 512 if nk == 8 else 128
                for u in range(1024 // ncol):
                    wv, wr = load_w(wsrc(w_l, u * ncol, ncol), nk, ncol)
                    for jj in range(ncol // 128):
                        o = u * (ncol // 128) + jj
                        pb = next_ps()
                        mm(PS[pb][:, 0:N], [(wv[:, kc, jj * 128:(jj + 1) * 128], src_fn(kc)) for kc in range(nk)],
                           reads=[wr] + src_regs, psreg=f"ps{pb}")
                        P.op("dve", lambda e, pb=pb, o=o: e.tensor_tensor(out=XT[:, o, 0:N], in0=XT[:, o, 0:N], in1=PS[pb][:, 0:N], op=ALU.add),
                             reads=[f"ps{pb}", f"x{o}"], writes=[f"x{o}"])

            P.tag = "mixout"
            proj_residual(w_mix[l], lambda kc: MB[:, kc, 0:N], R("mb", 8), 8)

            P.tag = "attn"
            rmsnorm(tc, "g_xattn", l)
            for half in range(2):
                wv, wr = load_w(wsrc(w_q[l], half * 512, 512), 8, 512)
                for jj in range(4):
                    j = half * 4 + jj
                    pb = zmm(wv, wr, jj)
                    P.op("act", lambda e, pb=pb, j=j: e.activation(bf8(0, j, N), PS[pb][:, 0:N], AF.Identity, scale=1.0 / 16),
                         reads=[f"ps{pb}"], writes=[bf8r(0, j)])
            for sg in segs:
                c0, n = sg.c0, sg.n
                if sg.kind == "p":
                    for h in range(4):
                        for mc in range(2):
                            pb = next_ps()
                            mm(PS[pb][:, 0:n], [(KT[:, l, 2 * h + dc, mc * 128:(mc + 1) * 128], bf8(0, 2 * h + dc, N)[:, c0:c0 + n]) for dc in range(2)],
                               reads=["kt", bf8r(0, 2 * h)], psreg=f"ps{pb}")
                            P.op("act", lambda e, pb=pb, h=h, mc=mc, c0=c0, n=n: e.activation(bf8(1, 2 * h + mc, N)[:, c0:c0 + n], PS[pb][:, 0:n], AF.Exp),
                                 reads=[f"ps{pb}"], writes=[bf8r(1, 2 * h + mc)])
                        pd = next_ps()
                        mm(PS[pd][:, 0:n], [(ONES[:, 2, :], bf8(1, 2 * h + mc, N)[:, c0:c0 + n]) for mc in range(2)], reads=["ones", bf8r(1, 2 * h)], psreg=f"ps{pd}")
                        g = next_gs()
                        P.op("dve", lambda e, pd=pd, g=g, n=n: e.reciprocal(GS[g][:, 0:n], PS[pd][:, 0:n]), reads=[f"ps{pd}"], writes=[f"gs{g}"])
                        for dc in range(2):
                            j = 2 * h + dc
                            pb = next_ps()
                            mm(PS[pb][:, 0:n], [(VT[:, l, mc, j * 128:(j + 1) * 128], bf8(1, 2 * h + mc, N)[:, c0:c0 + n]) for mc in range(2)],
                               reads=["vt", bf8r(1, 2 * h)], psreg=f"ps{pb}")
                            P.op("dve", lambda e, pb=pb, j=j, g=g, c0=c0, n=n: e.tensor_tensor(out=bf8(2, j, N)[:, c0:c0 + n], in0=PS[pb][:, 0:n], in1=GS[g][:, 0:n], op=ALU.mult),
                                 reads=[f"ps{pb}", f"gs{g}"], writes=[bf8r(2, j)])
                else:
                    ETS = STAT[1].bitcast(BF16)[:, 0:8 * n].rearrange("p (a x) -> p a x", a=8)
                    psc = next_ps()
                    pso = next_ps()
                    kvs = {}

                    def issue_kv(si, sg=sg):
                        if si >= sg.S or si in kvs:
                            return
                        s_ = sg.s0 + si
                        sl_ = cnt["slot"] % NSLOT
                        cnt["slot"] += 1
                        kv_ = WS[sl_][:, 0:2048].rearrange("p (j m) -> p j m", j=8)
                        vv_ = WS[sl_][:, 2048:4096].rearrange("p (c f) -> p c f", c=2)
                        P.dma("pool", lambda e, s_=s_, kv_=kv_: e.dma_start(out=kv_, in_=ckT[l, s_]), key=f"w{sl_}", writes=[f"ws{sl_}"])
                        P.dma("pool", lambda e, s_=s_, vv_=vv_: e.dma_start(out=vv_, in_=cvN[l, s_]), key=f"w{sl_}", writes=[f"ws{sl_}"])
                        kvs[si] = (kv_, vv_, f"ws{sl_}")

                    issue_kv(0)
                    issue_kv(1)
                    issue_kv(2)
                    for si in range(sg.S):
                        issue_kv(si + 3)
                        kv, vv, kvr = kvs[si]
                        for h in range(4):
                            for mc in range(2):
                                col = (h * 2 + mc) * n + si * TS
                                mm(PS[psc][:, col:col + TS],
                                   [(kv[:, 2 * h + dc, mc * 128:(mc + 1) * 128], bf8(0, 2 * h + dc, N)[:, c0 + si * TS:c0 + (si + 1) * TS]) for dc in range(2)],
                                   reads=[kvr, bf8r(0, 2 * h)], psreg=f"ps{psc}")
                        P.op("act", lambda e, si=si, n=n, ETS=ETS, psc=psc: e.activation(ETS[:, :, si * TS:(si + 1) * TS],
                                                                                 PS[psc][:, 0:8 * n].rearrange("p (a x) -> p a x", a=8)[:, :, si * TS:(si + 1) * TS], AF.Exp),
                             reads=[f"ps{psc}"], writes=["stat1"])
                        for j in range(8):
                            h = j // 2
                            col = j * n + si * TS
                            mm(PS[pso][:, col:col + TS],
                               [(vv[:, mc, j * 128:(j + 1) * 128], ETS[:, 2 * h + mc, si * TS:(si + 1) * TS]) for mc in range(2)],
                               reads=[kvr, "stat1"], psreg=f"ps{pso}")
                    pd = next_ps()
                    for h in range(4):
                        mm(PS[pd][:, h * n:(h + 1) * n], [(ONES[:, 2, :], ETS[:, 2 * h + mc, :]) for mc in range(2)],
                           reads=["ones", "stat1"], psreg=f"ps{pd}")
                    g = next_gs()
                    P.op("dve", lambda e, pd=pd, g=g, n=n: e.reciprocal(GS[g][:, 0:4 * n], PS[pd][:, 0:4 * n]), reads=[f"ps{pd}"], writes=[f"gs{g}"])
                    for j in range(8):
                        h = j // 2
                        P.op("dve", lambda e, j=j, h=h, g=g, c0=c0, n=n, pso=pso: e.tensor_tensor(out=bf8(2, j, N)[:, c0:c0 + n], in0=PS[pso][:, j * n:(j + 1) * n],
                                                                                        in1=GS[g][:, h * n:(h + 1) * n], op=ALU.mult),
                             reads=[f"ps{pso}", f"gs{g}"], writes=[bf8r(2, j)])
            P.tag = "xo"
            proj_residual(w_xo[l], lambda kc: bf8(2, kc, N), sblk(2), 8)

            P.tag = "ffn_up"
            rmsnorm(tc, "g_ffn", l, next_func=AF.Silu)
            for sg in segs:
                if sg.kind == "s":
                    P.dma("sp", lambda e, sg=sg: e.dma_start(out=HFS[:, sg.s0 // 8], in_=st_f[l, :, sg.s0 // 8]),
                          key="in_HFS", writes=["hfs"])

            def fin_ap(j):
                return bf8(2 + j // 8, j % 8, N), bf8r(2 + j // 8, j % 8)

            for u in range(11):
                s = cnt["slot"] % NSLOT
                cnt["slot"] += 1
                wv = WS[s][:, 0:4096].rearrange("p (k a n) -> p k a n", k=8, a=2)
                src = w_up[l].rearrange("(k p) (a n) -> p k a n", p=128, a=2)[:, :, :, u * 256:(u + 1) * 256]
                for a_ in range(2):
                    P.dma("pool", lambda e, wv=wv, src=src, a_=a_: e.dma_start(out=wv[:, :, a_, :], in_=src[:, :, a_, :]),
                          key=f"w{s}", writes=[f"ws{s}"])
                wr = f"ws{s}"
                conv_regs = {}
                for jj in range(2):
                    for a in range(2):
                        j = a * 22 + 2 * u + jj
                        pb = next_ps()
                        mm(PS[pb][:, 0:N], [(wv[:, kc, a, jj * 128:(jj + 1) * 128], HB[:, kc, 0:N]) for kc in range(8)],
                           reads=[wr] + R("hb", 8), psreg=f"ps{pb}")
                        r = cnt["scr"] % 4
                        cnt["scr"] += 1
                        P.op("act", lambda e, r=r, pb=pb, j=j: e.activation(flat(tc, 1, r), PS[pb][:, 0:N], AF.Identity,
                                                                            bias=ppc(l, "ffn_conv_b", j), scale=ppc(l, "ffn_conv_w", j * 3 + 2)),
                             reads=[f"ps{pb}", "pp"], writes=[sq_(1, r)])
                        for sg in segs:
                            T = sg.T
                            if sg.kind == "p":
                                hsrc, hreg = HF[:, l, j, :], f"hf{l}"
                            else:
                                hsrc, hreg = HFS[:, sg.s0 // 8, j, :, :], "hfs"
                            P.op("act", lambda e, r=r, hsrc=hsrc, sg=sg: e.copy(ext(sg, 0, r, 2, 0, 2), hsrc), reads=[hreg], writes=[sq_(0, r)])
                            P.op("act", lambda e, r=r, pb=pb, sg=sg, T=T: e.copy(ext(sg, 0, r, 2, 2, 2 + T), fseg(sg, PS[pb][:, 0:N])), reads=[f"ps{pb}"], writes=[sq_(0, r)])
                            P.op("act", lambda e, r=r, hsrc=hsrc, sg=sg, T=T: e.copy(hsrc, ext(sg, 0, r, 2, T, T + 2)), reads=[sq_(0, r)], writes=[hreg])
                        for k in range(2):
                            wk = ppc(l, "ffn_conv_w", j * 3 + k)
                            for sg in segs:
                                T = sg.T
                                P.op("dve", lambda e, r=r, wk=wk, k=k, sg=sg, T=T: e.scalar_tensor_tensor(out=fseg(sg, flat(tc, 1, r)), in0=ext(sg, 0, r, 2, k, k + T),
                                                                                                             scalar=wk, in1=fseg(sg, flat(tc, 1, r)),
                                                                                                             op0=ALU.mult, op1=ALU.add),
                                     reads=[sq_(0, r), sq_(1, r), "pp"], writes=[sq_(1, r)])
                        conv_regs[(jj, a)] = r
                    ra, rb = conv_regs[(jj, 0)], conv_regs[(jj, 1)]
                    g = next_gs()
                    P.op("act", lambda e, ra=ra, g=g: e.activation(GS[g][:, 0:N], flat(tc, 1, ra), AF.Silu), reads=[sq_(1, ra)], writes=[f"gs{g}"])
                    fa, fr = fin_ap(2 * u + jj)
                    P.op("dve", lambda e, rb=rb, g=g, fa=fa: e.tensor_tensor(out=fa, in0=GS[g][:, 0:N], in1=flat(tc, 1, rb), op=ALU.mult),
                         reads=[f"gs{g}", sq_(1, rb)], writes=[fr])
            for sg in segs:
                if sg.kind == "p":
                    if sg.last:
                        P.dma("sp", lambda e: e.dma_start(out=o_fp[l], in_=HF[:, l, :, :]), key="o_hf", reads=[f"hf{l}"])
                else:
                    P.dma("sp", lambda e, sg=sg: e.dma_start(out=o_fs[l, :, sg.s0 // 8], in_=HFS[:, sg.s0 // 8]),
                          key="o_HFS", reads=["hfs"])
            P.tag = "down"
            banks = [next_ps() for _ in range(8)]
            for ug in range(6):
                k0 = ug * 4
                nk_ = min(4, 22 - k0)
                src = w_down[l][k0 * 128:(k0 + nk_) * 128, :].rearrange("(k p) n -> p k n", p=128)
                wv, wr = load_w(src, nk_, 1024)
                for kk in range(nk_):
                    kc = k0 + kk
                    fa, fr = fin_ap(kc)

                    def fn(e, wv=wv, kk=kk, fa=fa, kc=kc):
                        ins = None
                        for o in range(8):
                            ins = e.matmul(PS[banks[o]][:, 0:N], wv[:, kk, o * 128:(o + 1) * 128], fa, start=(kc == 0), stop=(kc == 21))
                        return ins
                    P.op("pe", fn, reads=[wr, fr], writes=[f"ps{b}" for b in banks])
            for o in range(8):
                pb = banks[o]
                P.op("dve", lambda e, pb=pb, o=o: e.tensor_tensor(out=XT[:, o, 0:N], in0=XT[:, o, 0:N], in1=PS[pb][:, 0:N], op=ALU.add),
                     reads=[f"ps{pb}", f"x{o}"], writes=[f"x{o}"])

        tiles = [TC(i, [Seg("p", 1, 512, 0, t0=512 * i)]) for i in range(3)]
        tiles.append(TC(3, [Seg("p", 1, 256, 0, t0=1536), Seg("s", 8, TS, 256, s0=0, prevT=256)]))
        tiles.append(TC(4, [Seg("p", 1, 256, 0, t0=1792, last=True), Seg("s", 8, TS, 256, s0=8, prevT=256)]))
        if tile_sel is not None:
            tiles = [tiles[i] for i in tile_sel]
        for tc in tiles:
            N = tc.N
            for sg in tc.segs:
                if sg.kind == "p":
                    src = xpT[:, :, sg.t0:sg.t0 + sg.T]
                else:
                    src = xsT[:, :, sg.s0 * TS:(sg.s0 + sg.S) * TS]
                P.dma("sp", lambda e, src=src, sg=sg: e.dma_start(out=XT[:, :, sg.c0:sg.c0 + sg.n], in_=src), key="in_x", writes=R("x", 8))
            for l in range(nlayers):
                layer(tc, l)
            P.tag = "final"
            rmsnorm(tc, None, None, out_fn=lambda c: (flat(tc, c // 4, c % 4), sq_(c // 4, c % 4)))
            for c in range(8):
                for sg in tc.segs:
                    if sg.kind == "p":
                        dst = o_yp[:, c, sg.t0:sg.t0 + sg.T]
                    else:
                        dst = o_ys[:, c, sg.s0 * TS:(sg.s0 + sg.S) * TS]
                    P.dma("sp", lambda e, c=c, dst=dst, tc=tc, sg=sg: e.dma_start(out=dst, in_=flat(tc, c // 4, c % 4)[:, sg.c0:sg.c0 + sg.n]),
                          key=f"o_S{c // 4}", reads=[sq_(c // 4, c % 4)])
        P.wait_all("sp")
        P.emit()
    return nc


_CACHE = {}


def _fm(vec, nch):
    return np.ascontiguousarray(vec.reshape(nch, 128).T)


def prep_inputs(inp, ncores=8):
    f32 = np.float32
    g = {k: np.asarray(v, dtype=f32) for k, v in inp.items()}
    pp = np.zeros((128, NPP), f32)
    for l in range(NL):
        def put(name, arr):
            o = l * PPL + PPO[name]
            pp[:, o:o + arr.shape[1]] = arr
        put("g_mix", _fm(g["g_mix"][l], 8))
        put("g_xattn", _fm(g["g_xattn"][l], 8))
        put("g_ffn", _fm(g["g_ffn"][l], 8))
        put("conv_a_w", g["conv_a_w"][l].reshape(31, 4, 128).transpose(2, 1, 0).reshape(128, 124))
        put("conv_a_b", _fm(g["conv_a_b"][l], 4))
        put("ln_a_g", _fm(g["ln_a_g"][l], 4))
        put("ln_a_b", _fm(g["ln_a_b"][l], 4))
        put("conv_b_w", g["conv_b_w"][l].reshape(3, 4, 128).transpose(2, 1, 0).reshape(128, 12))
        put("conv_b_b", _fm(g["conv_b_b"][l], 4))
        put("ln_c_g", _fm(g["ln_c_g"][l], 4))
        put("ln_c_b", _fm(g["ln_c_b"][l], 4))
        put("scale_d", _fm(g["scale_d"][l], 4))
        put("ffn_conv_w", g["ffn_conv_w"][l].reshape(3, 44, 128).transpose(2, 1, 0).reshape(128, 132))
        put("ffn_conv_b", _fm(g["ffn_conv_b"][l], 44))
    pp[:, NL * PPL:NL * PPL + 8] = _fm(g["g_final"], 8)
    wsT = np.ascontiguousarray(g["w_s"].transpose(0, 3, 1, 2))
    small = g["w_s"][:, :, :TS, :TS].transpose(0, 3, 1, 2)
    wsS = np.ascontiguousarray(np.broadcast_to(small[:, None, :, :, None, :], (NL, NS, TS, 4, NS, TS)).reshape(NL, 64, 4, 64))
    bsb = np.ascontiguousarray(np.broadcast_to(g["b_s"][:, None, :, :], (NL, 128, 4, 128)))
    bss = np.ascontiguousarray(np.broadcast_to(g["b_s"][:, None, :, None, :TS], (NL, 128, 4, NS, TS)).reshape(NL, 128, 4, 64))
    wdg = np.ascontiguousarray(g["w_d_grp"].transpose(0, 2, 1, 3))
    shared = dict(pp=pp, wsT=wsT, wsS=wsS, bsb=bsb, bss=bss, wdg=wdg)
    for k in ("w_in", "w_a_out", "w_b_out", "w_c_out", "w_d_out", "w_mix_out", "w_q", "w_mk", "w_mv", "w_xo", "w_up", "w_down"):
        shared[k] = np.ascontiguousarray(g[k])

    def fmaj(x):
        t = x.shape[0]
        return np.ascontiguousarray(x.reshape(t, 8, 128).transpose(2, 1, 0))

    in_maps = []
    for b in range(ncores):
        sl = slice(b * NS, (b + 1) * NS)
        m = dict(shared)
        m["xpT"] = fmaj(g["x_prompt"][b])
        m["xsT"] = fmaj(g["x_sample"][sl].reshape(NS * TS, D))
        m["memT"] = fmaj(g["mem_prompt"][b])

        def st_fm(a, nch):
            l_, s_, r_, c_ = a.shape
            return np.ascontiguousarray(a.reshape(l_, s_, r_, nch, 128).transpose(0, 4, 3, 1, 2))
        m["st_a"] = st_fm(g["state_conv_a"][:, sl], 4)
        m["st_b"] = st_fm(g["state_conv_b"][:, sl], 4)
        m["st_d"] = st_fm(g["state_pool_d"][:, sl], 4)
        sf = st_fm(g["state_ffn_conv"][:, sl], 44)
        m["st_f"] = np.ascontiguousarray(sf.reshape(NL, 128, 44, 2, NS // 2, 2).transpose(0, 1, 3, 2, 4, 5))
        ck = g["cache_mem_k"][:, sl].reshape(NL, NS, 256, 8, 128)
        m["ckT"] = np.ascontiguousarray(ck.transpose(0, 1, 4, 3, 2))
        cv = g["cache_mem_v"][:, sl].reshape(NL, NS, 2, 128, 1024)
        m["cvN"] = np.ascontiguousarray(cv.transpose(0, 1, 3, 2, 4))
        in_maps.append(m)
    return in_maps


def kernel(**inp):
    ncores = 8
    in_maps = prep_inputs(inp, ncores)
    if "nc" not in _CACHE:
        _CACHE["nc"] = build_program()
    nc = _CACHE["nc"]
    res = run_bass_kernel_spmd(nc, in_maps, core_ids=list(range(ncores)))
    return assemble(res.results)


def assemble(rs):
    f32 = np.float32

    def tokmaj(a):
        return a.transpose(2, 1, 0).reshape(a.shape[2], -1)

    def st_back(a):
        l_, p_, c_, r_ = a.shape
        return a.transpose(0, 3, 2, 1).reshape(l_, r_, c_ * 128)

    def sts_back(a):
        l_, p_, c_, s_, r_ = a.shape
        return a.transpose(0, 3, 4, 2, 1).reshape(l_, s_, r_, c_ * 128)

    y_p = np.stack([tokmaj(r["o_yp"]) for r in rs]).astype(f32)
    y_s = np.concatenate([tokmaj(r["o_ys"]).reshape(NS, TS, D) for r in rs]).astype(f32)
    a_p = np.stack([st_back(r["o_ap"]) for r in rs], axis=1)
    b_p = np.stack([st_back(r["o_bp"]) for r in rs], axis=1)
    d_p = np.stack([st_back(r["o_dp"]) for r in rs], axis=1)
    f_p = np.stack([st_back(r["o_fp"]) for r in rs], axis=1)
    v_p = np.stack([st_back(r["o_vp"]) for r in rs], axis=1)
    mk_p = np.stack([r["o_mk"].transpose(0, 3, 2, 1).reshape(NL, 256, 4, 256) for r in rs], axis=1)
    mv_p = np.stack([r["o_mv"].transpose(0, 2, 1, 3).reshape(NL, 256, 4, 256) for r in rs], axis=1)
    a_s = np.concatenate([sts_back(r["o_as"]) for r in rs], axis=1)
    b_s = np.concatenate([sts_back(r["o_bs"]) for r in rs], axis=1)
    d_s = np.concatenate([sts_back(r["o_ds"]) for r in rs], axis=1)
    f_s = np.concatenate([sts_back(r["o_fs"].transpose(0, 1, 3, 2, 4, 5).reshape(NL, 128, 44, NS, 2)) for r in rs], axis=1)
    v_s = np.concatenate([r["o_vs"].reshape(NL, 128, 4, NS, TS).transpose(0, 3, 4, 2, 1).reshape(NL, NS, TS, 512) for r in rs], axis=1)
    outs = (y_p, y_s, a_p, b_p, d_p, f_p, v_p, mk_p, mv_p, a_s, b_s, d_s, f_s, v_s)
    return tuple(np.ascontiguousarray(o, dtype=f32) for o in outs)
```
